# Optimizing a Trainium2 kernel written in Bass

```python
import jax, jax.numpy as jnp
from jax import lax
import numpy as np

D_MODEL = 4096
BATCH = 4
SEQ = 4096
DEPTH = 4

PLE_DIM = 256
POOL_WINDOWS = (2, 4, 8, 16)
POOL_WIDTH = D_MODEL // 4
POOL_GROUP = POOL_WIDTH // len(POOL_WINDOWS)
SGU_WIDTH = D_MODEL // 4
SGU_HEADS = 8
SGU_HEAD_DIM = SGU_WIDTH // SGU_HEADS
CHUNK = 128
SB_WIDTH = D_MODEL - POOL_WIDTH - SGU_WIDTH
SB_HEAD_DIM = 128
SB_HEADS = SB_WIDTH // SB_HEAD_DIM
Q_BLOCK = 128
MIX_WIDTH = POOL_WIDTH + SGU_WIDTH + SB_WIDTH
IN_WIDTH = POOL_WIDTH + 2 * SGU_WIDTH + 3 * SB_WIDTH
D_FF = ((8 * D_MODEL // 3 + 255) // 256) * 256
RMS_EPS = 1e-6
LN_EPS = 1e-5

kernel_name = 'hybrid_pool_sgu_stickbreak_block'


def rms_norm(x, w):
    xf = x.astype(jnp.float32)
    y = xf * lax.rsqrt(jnp.mean(xf * xf, axis=-1, keepdims=True) + RMS_EPS)
    return (y * w.astype(jnp.float32)).astype(x.dtype)


def layer_norm(x, w):
    xf = x.astype(jnp.float32)
    xc = xf - jnp.mean(xf, axis=-1, keepdims=True)
    var = jnp.mean(xc * xc, axis=-1, keepdims=True)
    return (xc * lax.rsqrt(var + LN_EPS) * w.astype(jnp.float32)).astype(x.dtype)


def pool_mixer(a, pool_w, pool_scale):
    B, S, _ = a.shape
    wmax = max(POOL_WINDOWS)
    af = a.astype(jnp.float32)
    cs = jnp.cumsum(jnp.pad(af, ((0, 0), (wmax, 0), (0, 0))), axis=1)
    pos = jnp.arange(S)
    outs = []
    for g, w in enumerate(POOL_WINDOWS):
        lo, hi = g * POOL_GROUP, (g + 1) * POOL_GROUP
        csg = cs[..., lo:hi]
        win_sum = csg[:, wmax:] - csg[:, wmax - w: wmax - w + S]
        cnt = jnp.minimum(pos + 1, w).astype(jnp.float32)[None, :, None]
        outs.append(win_sum / cnt - af[..., lo:hi])
    pooled = jnp.stack(outs, axis=2).astype(a.dtype)
    y = jnp.einsum('bsgc,gcd->bsgd', pooled, pool_w).reshape(B, S, POOL_WIDTH)
    return y * pool_scale


def sgu_mixer(uv, norm_w, w_s, b_s):
    B, S, _ = uv.shape
    uv = jax.nn.gelu(uv)
    u, v = uv[..., :SGU_WIDTH], uv[..., SGU_WIDTH:]
    v = layer_norm(v, norm_w)
    nc = S // CHUNK
    v = v.reshape(B, nc, CHUNK, SGU_HEADS, SGU_HEAD_DIM)
    mask = jnp.tril(jnp.ones((CHUNK, CHUNK), dtype=bool))
    w_causal = jnp.where(mask, w_s, 0)
    mixed = jnp.einsum('hts,bnshd->bnthd', w_causal, v) + b_s.T[None, None, :, :, None]
    return u * mixed.reshape(B, S, SGU_WIDTH)


def stick_breaking_attention(q, k, v):
    B, S, H, Dh = q.shape
    scale = Dh ** -0.5
    nb = S // Q_BLOCK
    qb = q.reshape(B, nb, Q_BLOCK, H, Dh).transpose(1, 0, 2, 3, 4)
    key_pos = jnp.arange(S)

    def block(args):
        q_blk, blk_idx = args
        q_pos = blk_idx * Q_BLOCK + jnp.arange(Q_BLOCK)
        z = jnp.einsum('bthd,bshd->bhts', q_blk, k, preferred_element_type=jnp.float32) * scale
        causal = key_pos[None, :] < q_pos[:, None]
        log_beta = jax.nn.log_sigmoid(z)
        log_1m_beta = jnp.where(causal, -jax.nn.softplus(z), 0.0)
        suffix = lax.cumsum(log_1m_beta, axis=3, reverse=True) - log_1m_beta
        attn = jnp.where(causal, jnp.exp(log_beta + suffix), 0.0)
        return jnp.einsum('bhts,bshd->bthd', attn.astype(v.dtype), v)

    out = lax.map(block, (qb, jnp.arange(nb)))
    return out.transpose(1, 0, 2, 3, 4).reshape(B, S, H * Dh)


def setup_inputs(seed: int = 0) -> dict:
    key = jax.random.key(seed)
    ks = jax.random.split(key, 20)
    L = DEPTH

    def nrm(k, shape, fan_in):
        return jax.random.normal(k, shape, jnp.float32) * (fan_in ** -0.5)

    def gain(k, shape):
        return 1.0 + 0.05 * jax.random.normal(k, shape, jnp.float32)

    return {
        'x': jax.random.normal(ks[0], (BATCH, SEQ, D_MODEL), jnp.float32),
        'p': jax.random.normal(ks[1], (DEPTH, BATCH, SEQ, PLE_DIM), jnp.float32),
        'norm_mix_w': gain(ks[2], (L, D_MODEL)),
        'w_in': nrm(ks[3], (L, D_MODEL, IN_WIDTH), D_MODEL),
        'pool_w': nrm(ks[4], (L, len(POOL_WINDOWS), POOL_GROUP, POOL_GROUP), POOL_GROUP),
        'pool_scale': gain(ks[5], (L, POOL_WIDTH)),
        'sgu_norm_w': gain(ks[6], (L, SGU_WIDTH)),
        'sgu_w': nrm(ks[7], (L, SGU_HEADS, CHUNK, CHUNK), CHUNK),
        'sgu_b': 1.0 + 0.1 * jax.random.normal(ks[8], (L, SGU_HEADS, CHUNK), jnp.float32),
        'w_out': nrm(ks[9], (L, MIX_WIDTH, D_MODEL), MIX_WIDTH),
        'norm_ffn_w': gain(ks[10], (L, D_MODEL)),
        'w_gate': nrm(ks[11], (L, D_MODEL, D_FF), D_MODEL),
        'w_up': nrm(ks[12], (L, D_MODEL, D_FF), D_MODEL),
        'w_down': nrm(ks[13], (L, D_FF, D_MODEL), D_FF),
        'norm_ple_w': gain(ks[14], (L, D_MODEL)),
        'ple_gate_down': nrm(ks[15], (L, D_MODEL, PLE_DIM), D_MODEL),
        'ple_gate_up': nrm(ks[16], (L, PLE_DIM, D_MODEL), PLE_DIM),
        'ple_proj': nrm(ks[17], (L, PLE_DIM, D_MODEL), PLE_DIM),
        'final_norm_w': gain(ks[18], (D_MODEL,)),
    }


def reference(x, p, norm_mix_w, w_in, pool_w, pool_scale, sgu_norm_w, sgu_w, sgu_b,
              w_out, norm_ffn_w, w_gate, w_up, w_down, norm_ple_w, ple_gate_down,
              ple_gate_up, ple_proj, final_norm_w):
    B, S, _ = x.shape
    o_pool = POOL_WIDTH
    o_sgu = o_pool + 2 * SGU_WIDTH
    h = x
    for i in range(DEPTH):
        xn = rms_norm(h, norm_mix_w[i])
        z = xn @ w_in[i]
        y_pool = pool_mixer(z[..., :o_pool], pool_w[i], pool_scale[i])
        y_sgu = sgu_mixer(z[..., o_pool:o_sgu], sgu_norm_w[i], sgu_w[i], sgu_b[i])
        q, k, v = jnp.split(z[..., o_sgu:], 3, axis=-1)
        q = q.reshape(B, S, SB_HEADS, SB_HEAD_DIM)
        k = k.reshape(B, S, SB_HEADS, SB_HEAD_DIM)
        v = v.reshape(B, S, SB_HEADS, SB_HEAD_DIM)
        y_sb = stick_breaking_attention(q, k, v)
        y = jnp.concatenate([y_pool, y_sgu, y_sb], axis=-1)
        h = h + y @ w_out[i]
        hn = rms_norm(h, norm_ffn_w[i])
        h = h + (jax.nn.silu(hn @ w_gate[i]) * (hn @ w_up[i])) @ w_down[i]
        gn = rms_norm(h, norm_ple_w[i])
        gate = jax.nn.sigmoid((gn @ ple_gate_down[i]) @ ple_gate_up[i])
        h = h + gate * (p[i].astype(h.dtype) @ ple_proj[i])
    return rms_norm(h, final_norm_w)
```

```python
import numpy as np
from contextlib import ExitStack
import concourse.bass as bass
import concourse.mybir as mybir
from concourse.bass_utils import run_bass_kernel_spmd

F32 = mybir.dt.float32
BF16 = mybir.dt.bfloat16
AF = mybir.ActivationFunctionType
ALU = mybir.AluOpType

DEPTH = 4
D = 4096
SEQ = 4096
NTOK = 2048
NBLK = 16
D_FF = 11008
IN_W = 9216
PLE = 256
RMS_EPS = 1e-6
LN_EPS = 1e-5
ATT_SCALE = 128 ** -0.5

ENGINES = ("pe", "act", "dve", "pool", "sp")
SEM_ROT = 16000
KSTOP = 99
KDBG = 0


class Op:
    __slots__ = ("eng", "fn", "deps", "is_dma", "dsem", "dval", "sig", "sigidx", "idx", "dinc", "nobar")


class Prog:
    def __init__(self, nc):
        self.nc = nc
        self.ops = {e: [] for e in ENGINES}
        self.last_w = {}
        self.readers = {}
        self.dma_tot = {}
        self.bar = {e: [] for e in ENGINES}
        self.out_dmas = []
        self.last_op = {}

    def _mk(self, eng, fn, reads, writes, extra, isdma=False, nobar=False):
        op = Op()
        op.eng = eng
        op.fn = fn
        op.is_dma = False
        op.sig = False
        op.dsem = None
        op.dval = 0
        op.dinc = 16
        op.nobar = nobar
        deps = []
        for r in reads:
            w = self.last_w.get(r)
            if w is not None:
                deps.append(w)
        for r in writes:
            w = self.last_w.get(r)
            if w is not None:
                deps.append(w)
            rs = self.readers.get(r)
            if rs:
                deps.extend(rs.values())
        deps.extend(extra)
        if not nobar and self.bar[eng]:
            deps.extend(self.bar[eng])
            self.bar[eng] = []
        op.deps = deps
        op.idx = len(self.ops[eng])
        for r in reads:
            rd = self.readers.get(r)
            if rd is None:
                rd = self.readers[r] = {}
            rd[eng if not isdma else (eng, op.idx)] = op
        for r in writes:
            self.last_w[r] = op
            self.readers[r] = {}
        self.ops[eng].append(op)
        if not isdma:
            self.last_op[eng] = op
        return op

    def op(self, eng, fn, reads=(), writes=(), extra=()):
        return self._mk(eng, fn, reads, writes, extra)

    def dma(self, eng, out, in_, semkey, reads=(), writes=(), extra=(), fn=None, inc=16, nobar=False, kw=None):
        if fn is None:
            kw = kw or {}

            def fn(e):
                return e.dma_start(out=out, in_=in_, **kw)
        op = self._mk(eng, fn, reads, writes, extra, isdma=True, nobar=nobar)
        op.is_dma = True
        op.dsem = semkey
        op.dinc = inc
        tot = self.dma_tot.get(semkey, 0) + inc
        self.dma_tot[semkey] = tot
        op.dval = tot
        if not nobar:
            self.out_dmas.append(op)
        return op

    def barrier(self):
        deps = list(self.last_op.values()) + self.out_dmas
        self.out_dmas = []
        for e in ENGINES:
            self.bar[e] = list(deps)

    def plan(self):
        for e in ENGINES:
            for op in self.ops[e]:
                for d in op.deps:
                    if not d.is_dma:
                        if d.eng == "pe" and op.eng == "pe" and not op.is_dma:
                            continue
                        d.sig = True
        nsig = {}
        for e in ENGINES:
            c = 0
            for op in self.ops[e]:
                if op.is_dma:
                    continue
                if op.sig:
                    c += 1
                    op.sigidx = c
            nsig[e] = c
        names = []
        for e in ENGINES:
            for r in range((nsig[e] + SEM_ROT - 1) // SEM_ROT):
                names.append("p_%s_%d" % (e, r))
        for k in self.dma_tot:
            names.append("d_" + str(k))
        return names

    def emit(self, block, sem_objs):
        def make_stream(e):
            ops = self.ops[e]

            def body(eng):
                waited = {}
                for op in ops:
                    for d in op.deps:
                        if d.is_dma:
                            sname = "d_" + str(d.dsem)
                            val = d.dval
                        else:
                            if d.eng == "pe" and e == "pe" and not op.is_dma:
                                continue
                            k = d.sigidx
                            sname = "p_%s_%d" % (d.eng, (k - 1) // SEM_ROT)
                            val = (k - 1) % SEM_ROT + 1
                        if waited.get(sname, 0) >= val:
                            continue
                        waited[sname] = val
                        eng.wait_ge(sem_objs[sname], val)
                    ins = op.fn(eng)
                    if op.is_dma:
                        ins.then_inc(sem_objs["d_" + str(op.dsem)], op.dinc)
                    elif op.sig:
                        k = op.sigidx
                        ins.then_inc(sem_objs["p_%s_%d" % (e, (k - 1) // SEM_ROT)], 1)
            return body

        block.tensor(make_stream("pe"))
        block.scalar(make_stream("act"))
        block.vector(make_stream("dve"))
        block.gpsimd(make_stream("pool"))
        block.sync(make_stream("sp"))


def build_program(depth=DEPTH):
    nc = bass.Bass("TRN2", target_bir_lowering=False)
    P = Prog(nc)
    uid = [0]

    def din(name, shape, dt=F32):
        return nc.dram_tensor(name, shape, dt, kind="ExternalInput").ap()

    def dscr(name, shape, dt):
        if KDBG and name in ("hbuf", "sguU", "sguV", "qT", "yT", "hidT", "g1T"):
            return nc.dram_tensor(name, shape, dt, kind="ExternalOutput")
        return nc.dram_tensor(name, shape, dt)

    x = din("x", [NTOK, D])
    pT = din("pT", [DEPTH * PLE, NTOK])
    MI = 1 << 20
    NCH = 23
    WOFF = {}
    GROUPS = (("B", 12, (("w_in", (D, IN_W)), ("w_out", (D, D)), ("w_down", (D_FF, D)), ("ple_proj", (PLE, D)))),
              ("A", 11, (("w_gate", (D, D_FF)), ("w_up", (D, D_FF)), ("ple_gate_down", (D, PLE)), ("ple_gate_up", (PLE, D)))))
    CH_BUF = []
    for gname, nch, lst in GROUPS:
        off = 0
        for nm, (R_, C_) in lst:
            WOFF[nm] = (gname, off, R_, C_)
            off += R_ * C_
        assert off == nch * 8 * MI
        CH_BUF += [(gname, i) for i in range(nch)]
    wsh = din("wsh", [DEPTH * NCH * 512, 2048])
    wlocs = {}
    wbig = {}
    for l in range(depth):
        for gname, nch, lst in GROUPS:
            wbig[l, gname] = dscr("wbig%s%d" % (gname, l), [nch * 4096, 2048], BF16)
        for j in range(NCH):
            wlocs[l, j] = dscr("wloc%d_%d" % (l, j), [512, 2048], BF16)
    wfull = {}
    for l in range(depth):
        for nm, (gname, o_, R_, C_) in WOFF.items():
            flat = wbig[l, gname].ap().rearrange("a b -> (a b)")
            wfull[nm, l] = flat[o_:o_ + R_ * C_].rearrange("(r c) -> r c", c=C_)
    pool_w = din("pool_w", [DEPTH * 4 * 256, 256])
    pool_scale = din("pool_scale", [DEPTH, 1024])
    sgu_nw = din("sgu_norm_w", [DEPTH, 1024])
    sgu_w = din("sgu_w", [DEPTH * 8 * 128, 128])
    sgu_b = din("sgu_b", [DEPTH, 1024])
    nmix = din("norm_mix_w", [DEPTH, D])
    nffn = din("norm_ffn_w", [DEPTH, D])
    nple = din("norm_ple_w", [DEPTH, D])
    nfin = din("final_norm_w", [1, D])
    c_ident = din("c_ident", [128, 128])
    c_tril = din("c_tril", [128, 128])
    c_m2 = din("c_m2", [128, 256])
    c_sel = din("c_sel", [128, 2])
    c_invc = din("c_invc", [128, 4 * 128])
    out = nc.dram_tensor("out", [NTOK, D], F32, kind="ExternalOutput").ap()

    hbuf = dscr("hbuf", [NTOK, D], F32).ap()
    zpool_loc = dscr("zpool_loc", [1024, NTOK], F32)
    zh_loc = dscr("zh_loc", [1024, 1024], BF16)
    zh_all = dscr("zh_all", [2048, 1024], BF16)
    sguU = dscr("sguU", [1024, NTOK], F32).ap()
    sguV = dscr("sguV", [NTOK, 1024], F32).ap()
    qT = dscr("qT", [2048, NTOK], BF16).ap()
    kT_loc = [dscr("kT_loc%d" % q, [512, NTOK], BF16) for q in range(4)]
    kT_all = [dscr("kT_all%d" % q, [1024, NTOK], BF16) for q in range(4)]
    v_loc = [dscr("v_loc%d" % q, [512, 2048], BF16) for q in range(4)]
    v_all = [dscr("v_all%d" % q, [1024, 2048], BF16) for q in range(4)]
    yT = dscr("yT", [D, NTOK], BF16).ap()
    hidT = dscr("hidT", [D_FF, NTOK], BF16).ap()
    g1T = dscr("g1T", [PLE, NTOK], BF16).ap()

    A = nc.alloc_sbuf_tensor
    NSLAB = 6
    ring = [A("ring%d" % i, [128, 8, 512], BF16).ap() for i in range(NSLAB)]
    ident = A("ident", [128, 128], BF16).ap()
    tril = A("tril", [128, 128], F32).ap()
    m2 = A("m2", [128, 256], F32).ap()
    sel = A("sel", [128, 2], F32).ap()
    invc = A("invc", [128, 4, 128], F32).ap()
    epsr = A("epsr", [128, 1], F32).ap()
    epsl = A("epsl", [128, 1], F32).ap()
    oneb = A("oneb", [128, 1], F32).ap()
    pscol = A("pscol", [128, DEPTH, 8], F32).ap()
    banks = [nc.alloc_psum_tensor("bank%d" % i, [128, 512], F32).ap() for i in range(8)]

    def BK(i):
        return ("bank", i)

    P.op("dve", lambda e: e.memset(epsr, RMS_EPS), writes=["epsr"])
    P.op("dve", lambda e: e.memset(epsl, LN_EPS), writes=["epsl"])
    P.op("dve", lambda e: e.memset(oneb, 1.0), writes=["oneb"])
    P.dma("pool", ident, c_ident, "c", writes=["ident"])
    P.dma("sp", tril, c_tril, "c", writes=["tril"])
    P.dma("sp", m2, c_m2, "c", writes=["m2"])
    P.dma("sp", sel, c_sel, "c", writes=["sel"])
    P.dma("sp", invc.rearrange("p g t -> p (g t)"), c_invc, "c", writes=["invc"])
    for l in range(depth):
        P.dma("sp", pscol[:, l, :], pool_scale[l:l + 1, :].rearrange("o (c p) -> p (o c)", p=128), "c",
              writes=[("pscol", l)], kw=dict(allow_slow_non_contiguous=True))
    P.barrier()

    for l in range(depth):
        for j in range(NCH):
            P.dma("pool", wlocs[l, j].ap(), wsh[(l * NCH + j) * 512:(l * NCH + j + 1) * 512, :], "bc%d" % (j % 2),
                  writes=[("wloc", l, j)], nobar=True)
    all_wloc = [("wloc", l, j) for l in range(depth) for j in range(NCH)]

    def gather_weights(l):
        for j in range(NCH):
            P.dma("pool", None, None, "ccw", reads=all_wloc, writes=[("wbig", l, j)], inc=1, nobar=True,
                  fn=lambda e, l=l, j=j: e.collective_compute(
                      "AllGather", ALU.bypass, replica_groups=[list(range(8))], ins=[wlocs[l, j].ap().opt()],
                      outs=[wbig[l, CH_BUF[j][0]].ap()[CH_BUF[j][1] * 4096:(CH_BUF[j][1] + 1) * 4096, :].opt()]))
    gather_weights(0)

    ring_state = {"i": 0}
    grp_state = {"i": 0}

    def sb(st, name, shape, dt):
        uid[0] += 1
        return st.enter_context(nc.sbuf_tensor("%s_%d" % (name, uid[0]), shape, dt)).ap()

    def copy_op(eng, o, i, reads, writes):
        if eng == "act":
            return P.op("act", lambda e: e.copy(out=o, in_=i), reads=reads, writes=writes)
        return P.op(eng, lambda e: e.tensor_copy(out=o, in_=i), reads=reads, writes=writes)

    def norm_T(st, src, row0, nb, nw_row, actT, akey):
        wB = sb(st, "wB", [128, D], F32)
        ht = [sb(st, "ht", [128, D], F32) for _ in range(2)]
        xn = [sb(st, "xn", [128, D], BF16) for _ in range(2)]
        ss = sb(st, "ss", [128, 4], F32)
        k_wB = ("wB", uid[0])
        P.dma("sp", wB, nw_row.partition_broadcast(128), "nw", writes=[k_wB])
        for b in range(nb):
            h = ht[b % 2]
            xb = xn[b % 2]
            hk = ("ht", b % 2)
            xk = ("xn", b % 2)
            P.dma("sp", h, src[row0 + b * 128: row0 + (b + 1) * 128, :], "ht%d" % (b % 2), writes=[hk])
            P.op("act", lambda e, h=h, xb=xb: e.activation(out=xb, in_=h, func=AF.Square, accum_out=ss[:, 0:1]),
                 reads=[hk], writes=[xk, "ss0"])
            P.op("act", lambda e: e.activation(out=ss[:, 1:2], in_=ss[:, 0:1], func=AF.Sqrt, bias=epsr[:, 0:1],
                                               scale=1.0 / D), reads=["ss0", "epsr"], writes=["ss1"])
            P.op("dve", lambda e: e.reciprocal(out=ss[:, 2:3], in_=ss[:, 1:2]), reads=["ss1"], writes=["ss2"])
            P.op("dve", lambda e, h=h, xb=xb: e.scalar_tensor_tensor(out=xb, in0=h, scalar=ss[:, 2:3], in1=wB,
                                                                      op0=ALU.mult, op1=ALU.mult),
                 reads=[hk, "ss2", k_wB], writes=[xk])
            for g in range(8):
                bk = g % 2
                pt = banks[bk].bitcast(BF16)[:, 0:512].rearrange("p (c t) -> p c t", c=4)
                for c in range(4):
                    cc = g * 4 + c
                    P.op("pe", lambda e, pt=pt, c=c, cc=cc, xb=xb: e.transpose(pt[:, c, :], xb[:, cc * 128:(cc + 1) * 128], ident),
                         reads=[xk, "ident"], writes=[BK(bk)])
                dst = actT[:, g * 4:(g + 1) * 4, b * 128:(b + 1) * 128]
                copy_op("act" if g % 2 == 0 else "dve", dst, pt, [BK(bk)], [(akey, b, g)])

    def linear(actT, akey, KC, TT, panels):
        nslab = (KC + 7) // 8
        ngrp = TT // 512
        assert ngrp == 1 or nslab <= NSLAB - 1
        for pn in panels:
            mode = pn["mode"]
            ncols = pn["ncols"]
            slabs = {}

            def load_slab(s, pn=pn, slabs=slabs):
                r = ring_state["i"] % NSLAB
                ring_state["i"] += 1
                rg = ring[r]
                kcs = min(8, KC - s * 8)
                for hf in range(0, kcs, 4):
                    nk = min(4, kcs - hf)
                    for si, (src, coff) in enumerate(pn["srcs"]):
                        w = src.shape[1]
                        k0 = (s * 8 + hf) * 128
                        sa = src[k0:k0 + nk * 128, :].rearrange("(kc p) n -> p kc n", p=128)
                        P.dma("pool", rg[:, hf:hf + nk, coff:coff + w], sa, "rg%d_%d" % (r, hf // 4),
                              reads=[("wbig", pn["wkeys"][si][2], j_) for j_ in range(NCH)], writes=[("ring", r, hf // 4, si)], nobar=True)
                slabs[s] = (r, rg, kcs)

            for g in range(ngrp):
                bk0 = (grp_state["i"] % 2) * 4
                grp_state["i"] += 1
                nj = 4 if mode == "tok" else ncols // 128
                for s in range(nslab):
                    if g == 0:
                        load_slab(s)
                    r, rg, kcs = slabs[s]
                    for kc in range(kcs):
                        kk = s * 8 + kc
                        rk = [("ring", r, kc // 4, 0), ("ring", r, kc // 4, 1)]
                        for j in range(nj):
                            if mode == "tok":
                                lhsT = actT[:, kk, g * 512 + j * 128: g * 512 + (j + 1) * 128]
                                rhs = rg[:, kc, 0:ncols]
                                o = banks[bk0 + j][:, 0:ncols]
                            else:
                                lhsT = rg[:, kc, j * 128:(j + 1) * 128]
                                rhs = actT[:, kk, g * 512:(g + 1) * 512]
                                o = banks[bk0 + j]
                            P.op("pe", lambda e, o=o, l=lhsT, rh=rhs, st_=(kk == 0), sp_=(kk == KC - 1):
                                 e.matmul(o, l, rh, start=st_, stop=sp_), reads=rk, writes=[BK(bk0 + j)])
                pn["epi"](pn, g, bk0)

    def make_ev(st, n, shape, dt, name):
        tiles = [sb(st, name, shape, dt) for _ in range(n)]
        cnt = [0]
        base = uid[0]

        def nxt():
            i = cnt[0] % n
            cnt[0] += 1
            return tiles[i], (name, base, i)
        return nxt

    def epi_store_feat(nxt, dst, row0, tok0, dt_eng=("act", "dve")):
        def epi(pn, g, bk0):
            for j in range(pn["ncols"] // 128):
                t, k = nxt()
                copy_op(dt_eng[j % 2], t, banks[bk0 + j], [BK(bk0 + j)], [k])
                rr = row0 + pn["n0"] + j * 128
                if callable(dst):
                    d_, rr = dst(rr)
                else:
                    d_ = dst
                P.dma("sp", d_[rr: rr + 128, tok0 + g * 512: tok0 + (g + 1) * 512], t, "st_%s_%d" % (k[0], k[2]), reads=[k])
        return epi

    def epi_store_tok(nxt, dst, col0, tok0):
        def epi(pn, g, bk0):
            nco = pn["ncols"]
            for j in range(4):
                t, k = nxt()
                copy_op(("act", "dve")[j % 2], t[:, 0:nco], banks[bk0 + j][:, 0:nco], [BK(bk0 + j)], [k])
                r0 = tok0 + g * 512 + j * 128
                if callable(dst):
                    d_, r0 = dst(r0)
                else:
                    d_ = dst
                P.dma("sp", d_[r0:r0 + 128, col0 + pn["n0"]: col0 + pn["n0"] + nco], t[:, 0:nco],
                      "st_%s_%d" % (k[0], k[2]), reads=[k])
        return epi

    def epi_resid(nxt_h, hsrc, tok0):
        def epi(pn, g, bk0):
            nco = pn["ncols"]
            for j in range(4):
                t, k = nxt_h()
                r0 = tok0 + g * 512 + j * 128
                c0 = pn["n0"]
                P.dma("sp", t[:, 0:nco], hsrc[r0:r0 + 128, c0:c0 + nco], "ld_%s_%d" % (k[0], k[2]), writes=[k])
                P.op("dve", lambda e, t=t, b=banks[bk0 + j], nco=nco: e.tensor_tensor(out=t[:, 0:nco], in0=b[:, 0:nco],
                                                                                      in1=t[:, 0:nco], op=ALU.add),
                     reads=[BK(bk0 + j), k], writes=[k])
                P.dma("sp", hbuf[r0:r0 + 128, c0:c0 + nco], t[:, 0:nco], "st_%s_%d" % (k[0], k[2]), reads=[k])
        return epi

    for l in range(depth):
        hsrc = x if l == 0 else hbuf

        for ps in range(2 if KSTOP >= 0 else 0):
            tok0 = ps * 1024
            with ExitStack() as st:
                actT = sb(st, "actT", [128, 32, 1024], BF16)
                with ExitStack() as st2:
                    norm_T(st2, hsrc, tok0, 8, nmix[l:l + 1, :], actT, "actT")
                    P.barrier()
                ev32 = make_ev(st, 4, [128, 512], F32, "ev32")
                ev16 = make_ev(st, 4, [128, 512], BF16, "ev16")
                W = wfull["w_in", l]
                panels = []

                def add(c0, n, mode, epi, n0base):
                    for i in range(n):
                        panels.append(dict(srcs=[(W[:, c0 + i * 512: c0 + (i + 1) * 512], 0)], ncols=512, mode=mode,
                                           epi=epi, n0=n0base + i * 512, wkeys=[("wfull", "w_in", l)]))
                add(0, 2, "feat", epi_store_feat(ev32, zpool_loc.ap(), 0, tok0), 0)
                add(1024, 2, "feat", epi_store_feat(ev32, sguU, 0, tok0), 0)
                add(2048, 2, "tok", epi_store_tok(ev32, sguV, 0, tok0), 0)
                add(3072, 4, "feat", epi_store_feat(ev16, qT, 0, tok0), 0)
                add(5120, 4, "feat", epi_store_feat(ev16, lambda r: (kT_loc[r // 512].ap(), r % 512), 0, tok0), 0)
                add(7168, 4, "tok", epi_store_tok(ev16, lambda r: (v_loc[r // 512].ap(), r % 512), 0, tok0), 0)
                linear(actT, "actT", 32, 1024, panels)
                P.barrier()

        rgp = [[0, 1], [2, 3], [4, 5], [6, 7]]
        zhl = zh_loc.ap().bitcast(F32)
        for c8 in range(8):
            P.dma("sp", zhl[c8 * 128:(c8 + 1) * 128, :].rearrange("p (m t) -> p m t", m=NBLK),
                  zpool_loc.ap()[c8 * 128:(c8 + 1) * 128, :].rearrange("p (m t) -> p m t", m=NBLK)[:, :, 96:128],
                  "zh", writes=[("zhl", c8)])
        P.barrier()
        pairs = [("zp", zh_loc, zh_all)] + [("kT%d" % q, kT_loc[q], kT_all[q]) for q in range(4)] + \
                [("v%d" % q, v_loc[q], v_all[q]) for q in range(4)]
        cc_ops = []
        for nm, a, b_ in pairs:
            cc_ops.append(P.dma("pool", None, None, "cc_x", writes=[("cc", nm)], inc=1, nobar=False,
                                fn=lambda e, a=a, b_=b_: e.collective_compute("AllGather", ALU.bypass, replica_groups=rgp,
                                                                               ins=[a.ap().opt()], outs=[b_.ap().opt()])))
        del P.out_dmas[-len(pairs):]
        if l + 1 < depth:
            gather_weights(l + 1)

        if KSTOP < 2:
            continue
        with ExitStack() as st:
            wraw = sb(st, "wraw", [128, 8, 128], F32)
            wm = sb(st, "wm", [128, 8, 128], BF16)
            wcT = sb(st, "wcT", [128, 8, 128], BF16)
            bB = sb(st, "bB", [128, 1024], F32)
            nwB = sb(st, "nwB", [128, 1024], F32)
            P.dma("sp", wraw, sgu_w[l * 1024:(l + 1) * 1024, :].rearrange("(h t) s -> t h s", h=8), "sg",
                  writes=["wraw"])
            P.dma("sp", bB, sgu_b[l:l + 1, :].partition_broadcast(128), "sg", writes=["bB"])
            P.dma("sp", nwB, sgu_nw[l:l + 1, :].partition_broadcast(128), "sg", writes=["nwB"])
            for hh in range(8):
                P.op("dve", lambda e, hh=hh: e.tensor_tensor(out=wm[:, hh, :], in0=wraw[:, hh, :], in1=tril, op=ALU.mult),
                     reads=["wraw", "tril", "bB", "nwB"], writes=[("wm", hh)])
            for hg in range(2):
                pt = banks[hg].bitcast(BF16)[:, 0:512].rearrange("p (c t) -> p c t", c=4)
                for c in range(4):
                    hh = hg * 4 + c
                    P.op("pe", lambda e, pt=pt, c=c, hh=hh: e.transpose(pt[:, c, :], wm[:, hh, :], ident),
                         reads=[("wm", hh), "ident"], writes=[BK(hg)])
                copy_op("act", wcT[:, hg * 4:(hg + 1) * 4, :], pt, [BK(hg)], [("wcT", hg)])
            NB2 = 2
            vr = [sb(st, "vr", [128, 1024], F32) for _ in range(NB2)]
            ur = [sb(st, "ur", [128, 1024], F32) for _ in range(NB2)]
            t1 = [sb(st, "t1", [128, 1024], F32) for _ in range(NB2)]
            t2 = [sb(st, "t2", [128, 1024], F32) for _ in range(NB2)]
            vln = [sb(st, "vln", [128, 1024], BF16) for _ in range(NB2)]
            yo = [sb(st, "yo", [128, 1024], BF16) for _ in range(NB2)]
            stt = [sb(st, "stt", [128, 8], F32) for _ in range(NB2)]

            def gelu(xt, xk, ta, tak, tb, tbk, eng2="dve"):
                P.op("dve", lambda e: e.tensor_tensor(out=ta, in0=xt, in1=xt, op=ALU.mult), reads=[xk], writes=[tak])
                P.op("dve", lambda e: e.tensor_scalar(out=ta, in0=ta, scalar1=0.044715, scalar2=1.0, op0=ALU.mult,
                                                       op1=ALU.add), reads=[tak], writes=[tak])
                P.op("dve", lambda e: e.tensor_tensor(out=ta, in0=ta, in1=xt, op=ALU.mult), reads=[tak, xk], writes=[tak])
                P.op("act", lambda e: e.activation(out=tb, in_=ta, func=AF.Sigmoid, scale=1.5957691216057308),
                     reads=[tak], writes=[tbk])
                P.op("dve", lambda e: e.tensor_tensor(out=tb, in0=tb, in1=xt, op=ALU.mult), reads=[tbk, xk], writes=[tbk])

            for m in range(NBLK):
                i = m % NB2
                V, U, T1, T2, VL, YO, S = vr[i], ur[i], t1[i], t2[i], vln[i], yo[i], stt[i]
                kV, kU, kT1, kT2, kVL, kYO, kS = ("vr", i), ("ur", i), ("t1", i), ("t2", i), ("vln", i), ("yo", i), ("stt", i)
                P.dma("sp", V, sguV[m * 128:(m + 1) * 128, :], "sgl%d" % i, writes=[kV])
                P.dma("sp", U.rearrange("p (h t) -> p h t", h=8),
                      sguU[:, m * 128:(m + 1) * 128].rearrange("(h d) t -> d h t", h=8), "sgl%d" % i, writes=[kU])
                P.op("dve", lambda e, S=S: e.memset(S[:, 0:2], 0.0), reads=[kV, kU], writes=[(kS, 0), (kS, 1)])
                gelu(V, kV, T1, kT1, T2, kT2)
                P.op("act", lambda e, T1=T1, T2=T2, S=S: e.activation(out=T1, in_=T2, func=AF.Identity, accum_out=S[:, 0:1]),
                     reads=[kT2], writes=[kT1, (kS, 0)])
                P.op("act", lambda e, T1=T1, T2=T2, S=S: e.activation(out=T1, in_=T2, func=AF.Square, accum_out=S[:, 1:2]),
                     reads=[kT2], writes=[kT1, (kS, 1)])
                P.op("dve", lambda e, S=S: e.tensor_scalar(out=S[:, 2:3], in0=S[:, 0:1], scalar1=1.0 / 1024, scalar2=None,
                                                            op0=ALU.mult), reads=[(kS, 0)], writes=[(kS, 2)])
                P.op("dve", lambda e, S=S: e.tensor_tensor(out=S[:, 3:4], in0=S[:, 2:3], in1=S[:, 2:3], op=ALU.mult),
                     reads=[(kS, 2)], writes=[(kS, 3)])
                P.op("dve", lambda e, S=S: e.scalar_tensor_tensor(out=S[:, 4:5], in0=S[:, 1:2], scalar=1.0 / 1024,
                                                                   in1=S[:, 3:4], op0=ALU.mult, op1=ALU.subtract),
                     reads=[(kS, 1), (kS, 3)], writes=[(kS, 4)])
                P.op("act", lambda e, S=S: e.activation(out=S[:, 5:6], in_=S[:, 4:5], func=AF.Sqrt, bias=epsl[:, 0:1], scale=1.0),
                     reads=[(kS, 4), "epsl"], writes=[(kS, 5)])
                P.op("dve", lambda e, S=S: e.reciprocal(out=S[:, 6:7], in_=S[:, 5:6]), reads=[(kS, 5)], writes=[(kS, 6)])
                P.op("dve", lambda e, S=S: e.scalar_tensor_tensor(out=S[:, 7:8], in0=S[:, 2:3], scalar=-1.0, in1=S[:, 6:7],
                                                                   op0=ALU.mult, op1=ALU.mult),
                     reads=[(kS, 2), (kS, 6)], writes=[(kS, 7)])
                P.op("dve", lambda e, T1=T1, T2=T2, S=S: e.tensor_scalar(out=T1, in0=T2, scalar1=S[:, 6:7], scalar2=S[:, 7:8],
                                                                          op0=ALU.mult, op1=ALU.add),
                     reads=[kT2, (kS, 6), (kS, 7)], writes=[kT1])
                P.op("dve", lambda e, T1=T1, VL=VL: e.tensor_tensor(out=VL, in0=T1, in1=nwB, op=ALU.mult),
                     reads=[kT1, "nwB"], writes=[kVL])
                for hh in range(8):
                    bk = 2 + hh // 4
                    P.op("pe", lambda e, hh=hh, bk=bk, VL=VL: e.matmul(banks[bk][:, (hh % 4) * 128:(hh % 4 + 1) * 128],
                                                                        VL[:, hh * 128:(hh + 1) * 128], wcT[:, hh, :],
                                                                        start=True, stop=True),
                         reads=[kVL, ("wcT", hh // 4)], writes=[BK(bk)])
                gelu(U, kU, T1, kT1, T2, kT2)
                for hf in range(2):
                    P.op("dve", lambda e, hf=hf, T1=T1: e.tensor_tensor(out=T1[:, hf * 512:(hf + 1) * 512], in0=banks[2 + hf],
                                                                        in1=bB[:, hf * 512:(hf + 1) * 512], op=ALU.add),
                         reads=[BK(2 + hf), "bB"], writes=[kT1])
                P.op("dve", lambda e, T1=T1, T2=T2, YO=YO: e.tensor_tensor(out=YO, in0=T1, in1=T2, op=ALU.mult),
                     reads=[kT1, kT2], writes=[kYO])
                P.dma("sp", yT[1024:2048, m * 128:(m + 1) * 128].rearrange("(h d) t -> d h t", h=8),
                      YO.rearrange("p (h t) -> p h t", h=8), "sgy%d" % i, reads=[kYO])
            P.barrier()

        if KSTOP < 3:
            continue
        with ExitStack() as st:
            pw = sb(st, "pw", [128, 8, 256], BF16)
            for g in range(4):
                P.dma("pool", pw[:, g * 2:(g + 1) * 2, :],
                      pool_w[(l * 4 + g) * 256:(l * 4 + g + 1) * 256, :].rearrange("(cc p) d -> p cc d", p=128),
                      "pw", writes=[("pw", g)])
            pdT = sb(st, "pdT", [128, 8, NTOK], BF16)
            At = [sb(st, "pA", [128, NBLK, 144], F32) for _ in range(2)]
            Bt = [sb(st, "pB", [128, NBLK, 144], F32) for _ in range(2)]
            Ct = [sb(st, "pC", [128, NBLK, 144], F32) for _ in range(2)]
            cA = [sb(st, "cA", [128, NBLK, 16], F32) for _ in range(2)]
            cB = [sb(st, "cB", [128, NBLK, 16], F32) for _ in range(2)]
            for i in range(2):
                P.op("dve", lambda e, i=i: e.memset(cB[i], 0.0), writes=[("cB", i)])
            zl = zpool_loc.ap()
            za = zh_all.ap().bitcast(F32)
            for c in range(8):
                i = c % 2
                g = c // 2
                w = 2 << g
                a_, b_, c_ = At[i], Bt[i], Ct[i]
                kA, kB, kC, kcA, kcB = ("pA", i), ("pB", i), ("pC", i), ("cA", i), ("cB", i)
                P.dma("sp", a_[:, :, 16:144], zl[c * 128:(c + 1) * 128, :].rearrange("p (m t) -> p m t", m=NBLK),
                      "pl%d" % i, writes=[(kA, "main")])
                P.dma("sp", cA[i], za[c * 128:(c + 1) * 128, :].rearrange("p (m t) -> p m t", m=NBLK)[:, :, 16:32],
                      "pl%d" % i, writes=[kcA], extra=cc_ops)
                P.dma("sp", cB[i][:, 1:NBLK, :],
                      za[1024 + c * 128:1024 + (c + 1) * 128, :].rearrange("p (m t) -> p m t", m=NBLK)[:, 0:NBLK - 1, 16:32],
                      "pl%d" % i, writes=[kcB], extra=cc_ops)
                P.op("dve", lambda e, i=i: e.tensor_scalar(out=cA[i], in0=cA[i], scalar1=sel[:, 0:1], scalar2=None, op0=ALU.mult),
                     reads=["sel", kcA, kcB, (kA, "main")], writes=[kcA])
                P.op("dve", lambda e, i=i, a_=a_: e.scalar_tensor_tensor(out=a_[:, :, 0:16], in0=cB[i], scalar=sel[:, 1:2],
                                                                         in1=cA[i], op0=ALU.mult, op1=ALU.add),
                     reads=["sel", kcA, kcB], writes=[(kA, "halo")])
                cur, kcur, lo, span = a_, None, 0, 1
                bufs = [(b_, kB), (c_, kC)]
                step = 0
                rdk = [(kA, "main"), (kA, "halo")]
                while span < w:
                    nb_, nk_ = bufs[step % 2]
                    nlo = lo + span
                    P.op("dve", lambda e, cur=cur, nb_=nb_, nlo=nlo, span=span: e.tensor_tensor(
                        out=nb_[:, :, nlo:144], in0=cur[:, :, nlo:144], in1=cur[:, :, nlo - span:144 - span], op=ALU.add),
                        reads=rdk, writes=[nk_])
                    cur, lo, span = nb_, nlo, span * 2
                    rdk = [nk_]
                    step += 1
                P.op("dve", lambda e, cur=cur, a_=a_, c=c, w=w: e.scalar_tensor_tensor(
                    out=pdT[:, c, :].rearrange("p (m t) -> p m t", m=NBLK), in0=cur[:, :, 16:144], scalar=1.0 / w,
                    in1=a_[:, :, 16:144], op0=ALU.mult, op1=ALU.subtract),
                    reads=rdk + [(kA, "main")], writes=[("pdT", c)])
                fx, kfx = bufs[step % 2]
                P.op("dve", lambda e, cur=cur, fx=fx, g=g: e.tensor_tensor(out=fx[:, 0, 16:144], in0=cur[:, 0, 16:144],
                                                                           in1=invc[:, g, :], op=ALU.mult),
                     reads=rdk + ["invc"], writes=[kfx])
                P.op("dve", lambda e, fx=fx, a_=a_, c=c: e.tensor_tensor(out=pdT[:, c, 0:128], in0=fx[:, 0, 16:144],
                                                                         in1=a_[:, 0, 16:144], op=ALU.subtract),
                     reads=[kfx, (kA, "main"), ("pdT", c)], writes=[("pdT", c)])
            evp = make_ev(st, 4, [128, 512], BF16, "evp")
            gi = 0
            for g in range(4):
                for dc in range(2):
                    for tt in range(4):
                        bk = gi % 4
                        gi += 1
                        for cc in range(2):
                            P.op("pe", lambda e, g=g, dc=dc, tt=tt, cc=cc, bk=bk: e.matmul(
                                banks[bk], pw[:, g * 2 + cc, dc * 128:(dc + 1) * 128],
                                pdT[:, g * 2 + cc, tt * 512:(tt + 1) * 512], start=(cc == 0), stop=(cc == 1)),
                                reads=[("pw", g), ("pdT", g * 2 + cc)], writes=[BK(bk)])
                        t, k = evp()
                        P.op("act", lambda e, t=t, bk=bk, g=g, dc=dc, l=l: e.activation(out=t, in_=banks[bk], func=AF.Copy,
                                                                                  scale=pscol[:, l, g * 2 + dc: g * 2 + dc + 1]),
                             reads=[BK(bk), ("pscol", l)], writes=[k])
                        r0 = (g * 2 + dc) * 128
                        P.dma("sp", yT[r0:r0 + 128, tt * 512:(tt + 1) * 512], t, "st_%s_%d" % (k[0], k[2]), reads=[k])
            P.barrier()

        if KSTOP < 4:
            continue
        with ExitStack() as st:
            KT = sb(st, "KT", [128, 32, 128], BF16)
            VT = sb(st, "VT", [128, 32, 128], BF16)
            QT = sb(st, "QT", [128, NTOK], BF16)
            E_ = [sb(st, "E", [128, 4096], F32) for _ in range(2)]
            S_ = [sb(st, "S", [128, 4096], F32) for _ in range(2)]
            Pf = [sb(st, "Pf", [128, 4096], F32) for _ in range(2)]
            Ab = [sb(st, "Ab", [128, 4096], BF16) for _ in range(2)]
            ATt = [sb(st, "ATt", [128, 32, 128], BF16) for _ in range(2)]
            nt_ = [sb(st, "nt", [128, 2], F32) for _ in range(2)]
            oev = make_ev(st, 2, [128, 512], BF16, "oev")
            qi = 0
            for hh in range(16):
                ka = kT_all[hh // 4].ap()
                for r in range(2):
                    P.dma("sp", KT.rearrange("p (m r) t -> p m r t", r=2)[:, :, r, :],
                          ka[r * 512 + (hh % 4) * 128: r * 512 + (hh % 4 + 1) * 128, :].rearrange("p (m t) -> p m t", m=NBLK),
                          "kt", writes=[("KT", r)], extra=cc_ops)
                    for q in range(4):
                        va = v_all[q].ap()
                        P.dma("sp", VT.rearrange("p (m r) d -> p m r d", r=2)[:, q * 4:(q + 1) * 4, r, :],
                              va[r * 512:(r + 1) * 512, hh * 128:(hh + 1) * 128].rearrange("(m s) d -> s m d", s=128),
                              "vt", writes=[("VT", r, q)], extra=cc_ops)
                P.dma("sp", QT, qT[hh * 128:(hh + 1) * 128, :], "qt", writes=["QT"])
                for m in range(NBLK):
                    i = qi % 2
                    qi += 1
                    E, S, PF, AB, AT, NT = E_[i], S_[i], Pf[i], Ab[i], ATt[i], nt_[i]
                    kE, kS, kPF, kAB, kAT, kNT = ("E", i), ("S", i), ("Pf", i), ("Ab", i), ("ATt", i), ("nt", i)
                    nk = 2 * m + 2
                    ncol = nk * 128
                    ntile = (ncol + 511) // 512
                    for tl in range(ntile):
                        c0 = tl * 512
                        cw = min(512, ncol - c0)
                        bk = tl % 4
                        P.op("pe", lambda e, bk=bk, cw=cw, c0=c0, m=m: e.matmul(
                            banks[bk][:, 0:cw], QT[:, m * 128:(m + 1) * 128],
                            KT.rearrange("p g t -> p (g t)")[:, c0:c0 + cw], start=True, stop=True),
                            reads=["QT", ("KT", 0), ("KT", 1)], writes=[BK(bk)])
                        P.op("act", lambda e, bk=bk, cw=cw, c0=c0, E=E: e.activation(out=E[:, c0:c0 + cw], in_=banks[bk][:, 0:cw],
                                                                                      func=AF.Exp, scale=ATT_SCALE),
                             reads=[BK(bk)], writes=[kE])
                    P.op("dve", lambda e, E=E, ncol=ncol: e.tensor_tensor(out=E[:, ncol - 256:ncol], in0=E[:, ncol - 256:ncol],
                                                                          in1=m2, op=ALU.mult), reads=[kE, "m2"], writes=[kE])
                    P.op("act", lambda e, E=E, S=S, ncol=ncol: e.activation(out=S[:, 0:ncol], in_=E[:, 0:ncol], func=AF.Ln,
                                                                             bias=oneb[:, 0:1], scale=1.0), reads=[kE, "oneb"], writes=[kS])
                    P.op("dve", lambda e, S=S, PF=PF, ncol=ncol: e.tensor_tensor_scan(out=PF[:, 0:ncol], data0=S[:, 0:ncol],
                                                                                       data1=S[:, 0:ncol], initial=0.0,
                                                                                       op0=ALU.add, op1=ALU.max),
                         reads=[kS], writes=[kPF])
                    P.op("dve", lambda e, PF=PF, NT=NT, ncol=ncol: e.tensor_scalar(out=NT[:, 0:1], in0=PF[:, ncol - 1:ncol],
                                                                                    scalar1=-1.0, scalar2=None, op0=ALU.mult),
                         reads=[kPF], writes=[kNT])
                    P.op("dve", lambda e, S=S, PF=PF, ncol=ncol: e.tensor_tensor(out=S[:, 0:ncol], in0=PF[:, 0:ncol],
                                                                                 in1=S[:, 0:ncol], op=ALU.subtract),
                         reads=[kPF, kS], writes=[kS])
                    P.op("act", lambda e, S=S, PF=PF, NT=NT, ncol=ncol: e.activation(out=PF[:, 0:ncol], in_=S[:, 0:ncol],
                                                                                      func=AF.Exp, bias=NT[:, 0:1], scale=1.0),
                         reads=[kS, kNT], writes=[kPF])
                    P.op("dve", lambda e, E=E, PF=PF, AB=AB, ncol=ncol: e.tensor_tensor(out=AB[:, 0:ncol], in0=E[:, 0:ncol],
                                                                                        in1=PF[:, 0:ncol], op=ALU.mult),
                         reads=[kE, kPF], writes=[kAB])
                    for g4 in range((nk + 3) // 4):
                        bk = 4 + g4 % 2
                        n4 = min(4, nk - g4 * 4)
                        pt = banks[bk].bitcast(BF16)[:, 0:512].rearrange("p (c t) -> p c t", c=4)
                        for c in range(n4):
                            G = g4 * 4 + c
                            P.op("pe", lambda e, pt=pt, c=c, G=G, AB=AB: e.transpose(pt[:, c, :], AB[:, G * 128:(G + 1) * 128], ident),
                                 reads=[kAB, "ident"], writes=[BK(bk)])
                        copy_op("act" if g4 % 2 == 0 else "dve", AT[:, g4 * 4:g4 * 4 + n4, :], pt[:, 0:n4, :], [BK(bk)], [kAT])
                    bk = 6 + (hh * 4 + m // 4) % 2
                    for G in range(nk):
                        P.op("pe", lambda e, bk=bk, G=G, AT=AT, m=m: e.matmul(
                            banks[bk][:, (m % 4) * 128:(m % 4 + 1) * 128], VT[:, G, :], AT[:, G, :],
                            start=(G == 0), stop=(G == nk - 1)),
                            reads=[kAT] + [("VT", r_, q_) for r_ in range(2) for q_ in range(4)], writes=[BK(bk)])
                    if m % 4 == 3:
                        t, k = oev()
                        copy_op("dve", t, banks[bk], [BK(bk)], [k])
                        r0 = 2048 + hh * 128
                        P.dma("sp", yT[r0:r0 + 128, (m - 3) * 128:(m + 1) * 128], t, "st_%s_%d" % (k[0], k[2]), reads=[k])
            P.barrier()

        if KSTOP < 5:
            continue
        for ps in range(2):
            tok0 = ps * 1024
            with ExitStack() as st:
                actT = sb(st, "yTs", [128, 32, 1024], BF16)
                for q in range(8):
                    P.dma("sp", actT[:, q * 4:(q + 1) * 4, :],
                          yT[q * 512:(q + 1) * 512, tok0:tok0 + 1024].rearrange("(c p) t -> p c t", p=128),
                          "ya%d" % (q % 4), writes=[("yTs", b) for b in range(8)] if q == 0 else [("yTs_", q)])
                P.barrier()
                evh = make_ev(st, 6, [128, 512], F32, "evh")
                W = wfull["w_out", l]
                panels = [dict(srcs=[(W[:, i * 512:(i + 1) * 512], 0)], ncols=512, mode="tok",
                               epi=epi_resid(evh, hsrc, tok0), n0=i * 512, wkeys=[("wfull", "w_out", l)]) for i in range(8)]
                linear(actT, "yTs", 32, 1024, panels)
                P.barrier()

        if KSTOP < 6:
            continue
        for ps in range(2):
            tok0 = ps * 1024
            with ExitStack() as st:
                actT = sb(st, "hnT", [128, 32, 1024], BF16)
                with ExitStack() as st2:
                    norm_T(st2, hbuf, tok0, 8, nffn[l:l + 1, :], actT, "hnT")
                    P.barrier()
                sg = make_ev(st, 4, [128, 512], F32, "sg")
                hd = make_ev(st, 4, [128, 512], BF16, "hd")
                Wg = wfull["w_gate", l]
                Wu = wfull["w_up", l]

                def epi_ffn(pn, g, bk0, tok0=tok0):
                    f0 = pn["n0"]
                    for j in range(2):
                        s_, ks = sg()
                        h_, kh = hd()
                        P.op("act", lambda e, s_=s_, b=banks[bk0 + j]: e.activation(out=s_, in_=b, func=AF.Silu),
                             reads=[BK(bk0 + j)], writes=[ks])
                        P.op("dve", lambda e, s_=s_, h_=h_, b=banks[bk0 + 2 + j]: e.tensor_tensor(out=h_, in0=b, in1=s_, op=ALU.mult),
                             reads=[BK(bk0 + 2 + j), ks], writes=[kh])
                        P.dma("sp", hidT[f0 + j * 128:f0 + (j + 1) * 128, tok0 + g * 512:tok0 + (g + 1) * 512], h_,
                              "st_%s_%d" % (kh[0], kh[2]), reads=[kh])
                panels = [dict(srcs=[(Wg[:, f * 256:(f + 1) * 256], 0), (Wu[:, f * 256:(f + 1) * 256], 256)], ncols=512,
                               mode="feat", epi=epi_ffn, n0=f * 256,
                               wkeys=[("wfull", "w_gate", l), ("wfull", "w_up", l)]) for f in range(D_FF // 256)]
                linear(actT, "hnT", 32, 1024, panels)
                P.barrier()

        if KSTOP < 7:
            continue
        for ps in range(4):
            tok0 = ps * 512
            with ExitStack() as st:
                actT = sb(st, "hdT", [128, 86, 512], BF16)
                for q in range(22):
                    c0 = q * 4
                    n = min(4, 86 - c0)
                    P.dma("sp", actT[:, c0:c0 + n, :],
                          hidT[c0 * 128:(c0 + n) * 128, tok0:tok0 + 512].rearrange("(c p) t -> p c t", p=128),
                          "ya%d" % (q % 4), writes=[("hdT_", q)])
                P.barrier()
                evh = make_ev(st, 6, [128, 512], F32, "evh")
                W = wfull["w_down", l]
                panels = [dict(srcs=[(W[:, i * 512:(i + 1) * 512], 0)], ncols=512, mode="tok",
                               epi=epi_resid(evh, hbuf, tok0), n0=i * 512, wkeys=[("wfull", "w_down", l)]) for i in range(8)]
                linear(actT, "hdT", 86, 512, panels)
                P.barrier()

        if KSTOP < 8:
            continue
        for ps in range(2):
            tok0 = ps * 1024
            with ExitStack() as st:
                actT = sb(st, "gnT", [128, 32, 1024], BF16)
                with ExitStack() as st2:
                    norm_T(st2, hbuf, tok0, 8, nple[l:l + 1, :], actT, "gnT")
                    P.barrier()
                ev16 = make_ev(st, 4, [128, 512], BF16, "ev16")
                W = wfull["ple_gate_down", l]
                panels = [dict(srcs=[(W, 0)], ncols=256, mode="feat", epi=epi_store_feat(ev16, g1T, 0, tok0), n0=0,
                               wkeys=[("wfull", "ple_gate_down", l)])]
                linear(actT, "gnT", 32, 1024, panels)
                P.barrier()
        for ps in range(4):
            tok0 = ps * 512
            with ExitStack() as st:
                g1s = sb(st, "g1s", [128, 2, 512], BF16)
                pTs = sb(st, "pTs", [128, 2, 512], BF16)
                wu = sb(st, "wu", [128, 2, D], BF16)
                wp = sb(st, "wp", [128, 2, D], BF16)
                P.dma("sp", g1s, g1T[:, tok0:tok0 + 512].rearrange("(c p) t -> p c t", p=128), "ya0", writes=["g1s"])
                P.dma("pool", pTs, pT[l * PLE:(l + 1) * PLE, tok0:tok0 + 512].rearrange("(c p) t -> p c t", p=128), "pp",
                      writes=["pTs"])
                for c in range(2):
                    P.dma("pool", wu[:, c, :], wfull["ple_gate_up", l][c * 128:(c + 1) * 128, :], "pp",
                          reads=[("wbig", l, j_) for j_ in range(NCH)], writes=[("wu", c)])
                    P.dma("pool", wp[:, c, :], wfull["ple_proj", l][c * 128:(c + 1) * 128, :], "pp",
                          reads=[("wbig", l, j_) for j_ in range(NCH)], writes=[("wp", c)])
                sgt = make_ev(st, 2, [128, 512], F32, "sgt")
                evh = make_ev(st, 4, [128, 512], F32, "evh")
                for pn in range(8):
                    n0 = pn * 512
                    for j in range(4):
                        bka = (pn * 4 + j) % 4
                        bkb = 4 + bka
                        for c in range(2):
                            P.op("pe", lambda e, bka=bka, j=j, c=c, n0=n0: e.matmul(banks[bka], g1s[:, c, j * 128:(j + 1) * 128],
                                                                                   wu[:, c, n0:n0 + 512], start=(c == 0), stop=(c == 1)),
                                 reads=["g1s", "pTs", ("wu", 0), ("wu", 1), ("wp", 0), ("wp", 1)], writes=[BK(bka)])
                        for c in range(2):
                            P.op("pe", lambda e, bkb=bkb, j=j, c=c, n0=n0: e.matmul(banks[bkb], pTs[:, c, j * 128:(j + 1) * 128],
                                                                                   wp[:, c, n0:n0 + 512], start=(c == 0), stop=(c == 1)),
                                 reads=["pTs", ("wp", c)], writes=[BK(bkb)])
                        s_, ks = sgt()
                        t, k = evh()
                        r0 = tok0 + j * 128
                        P.dma("sp", t, hbuf[r0:r0 + 128, n0:n0 + 512], "ld_%s_%d" % (k[0], k[2]), writes=[k])
                        P.op("act", lambda e, s_=s_, bka=bka: e.activation(out=s_, in_=banks[bka], func=AF.Sigmoid),
                             reads=[BK(bka)], writes=[ks])
                        P.op("dve", lambda e, s_=s_, bkb=bkb: e.tensor_tensor(out=s_, in0=banks[bkb], in1=s_, op=ALU.mult),
                             reads=[BK(bkb), ks], writes=[ks])
                        P.op("dve", lambda e, s_=s_, t=t: e.tensor_tensor(out=t, in0=t, in1=s_, op=ALU.add),
                             reads=[ks, k], writes=[k])
                        P.dma("sp", hbuf[r0:r0 + 128, n0:n0 + 512], t, "st_%s_%d" % (k[0], k[2]), reads=[k])
                P.barrier()

    with ExitStack() as st:
        wB = sb(st, "fwB", [128, D], F32)
        ht = [sb(st, "fht", [128, D], F32) for _ in range(2)]
        ot = [sb(st, "fot", [128, D], F32) for _ in range(2)]
        ss = sb(st, "fss", [128, 4], F32)
        P.dma("sp", wB, nfin.partition_broadcast(128), "nw", writes=["fwB"])
        fin = []
        for b in range(NBLK):
            h, o = ht[b % 2], ot[b % 2]
            hk, ok = ("fht", b % 2), ("fot", b % 2)
            P.dma("sp", h, hbuf[b * 128:(b + 1) * 128, :], "ht%d" % (b % 2), writes=[hk])
            P.op("act", lambda e, h=h, o=o: e.activation(out=o, in_=h, func=AF.Square, accum_out=ss[:, 0:1]),
                 reads=[hk], writes=[ok, "fss0"])
            P.op("act", lambda e: e.activation(out=ss[:, 1:2], in_=ss[:, 0:1], func=AF.Sqrt, bias=epsr[:, 0:1], scale=1.0 / D),
                 reads=["fss0", "epsr"], writes=["fss1"])
            P.op("dve", lambda e: e.reciprocal(out=ss[:, 2:3], in_=ss[:, 1:2]), reads=["fss1"], writes=["fss2"])
            P.op("dve", lambda e, h=h, o=o: e.scalar_tensor_tensor(out=o, in0=h, scalar=ss[:, 2:3], in1=wB, op0=ALU.mult,
                                                                    op1=ALU.mult), reads=[hk, "fss2", "fwB"], writes=[ok])
            fin.append(P.dma("sp", out[b * 128:(b + 1) * 128, :], o, "fo%d" % (b % 2), reads=[ok]))
        P.op("sp", lambda e: e.nop(), extra=fin, writes=["fin"])

    names = P.plan()
    with ExitStack() as st:
        sems = {n: st.enter_context(nc.semaphore(n)) for n in names}
        block = st.enter_context(nc.Block())
        P.emit(block, sems)
    return nc, {e: len(P.ops[e]) for e in ENGINES}


def _consts(j):
    ident = np.eye(128, dtype=np.float32)
    tril = np.tril(np.ones((128, 128), np.float32))
    strict = np.tril(np.ones((128, 128), np.float32), -1)
    if j == 0:
        m2 = np.concatenate([strict, np.zeros((128, 128), np.float32)], 1)
        sel = np.tile(np.array([[0.0, 1.0]], np.float32), (128, 1))
    else:
        m2 = np.concatenate([np.ones((128, 128), np.float32), strict], 1)
        sel = np.tile(np.array([[1.0, 0.0]], np.float32), (128, 1))
    invc = np.zeros((4, 128), np.float32)
    pos = np.arange(128) + (0 if j == 0 else 128)
    for g, w in enumerate((2, 4, 8, 16)):
        invc[g] = 1.0 / np.minimum(pos + 1, w)
    invc = np.tile(invc.reshape(1, 512), (128, 1))
    return dict(c_ident=ident, c_tril=tril, c_m2=np.ascontiguousarray(m2), c_sel=sel, c_invc=np.ascontiguousarray(invc))


_CACHE = {}


def kernel(x, p, norm_mix_w, w_in, pool_w, pool_scale, sgu_norm_w, sgu_w, sgu_b, w_out, norm_ffn_w, w_gate, w_up,
           w_down, norm_ple_w, ple_gate_down, ple_gate_up, ple_proj, final_norm_w, _depth=DEPTH):
    f = lambda a: np.ascontiguousarray(np.asarray(a, dtype=np.float32))
    x = f(x)
    p = f(p)
    B = x.shape[0]
    if _depth not in _CACHE:
        _CACHE[_depth] = build_program(_depth)[0]
    nc = _CACHE[_depth]
    small = dict(
        pool_w=f(pool_w).reshape(DEPTH * 4 * 256, 256), pool_scale=f(pool_scale), sgu_norm_w=f(sgu_norm_w),
        sgu_w=f(sgu_w).reshape(DEPTH * 8 * 128, 128), sgu_b=f(sgu_b).reshape(DEPTH, 1024),
        norm_mix_w=f(norm_mix_w), norm_ffn_w=f(norm_ffn_w), norm_ple_w=f(norm_ple_w),
        final_norm_w=f(final_norm_w).reshape(1, D))
    MI = 1 << 20
    catB = np.concatenate([f(a).reshape(DEPTH, -1) for a in (w_in, w_out, w_down, ple_proj)], axis=1).reshape(DEPTH, 12, 8, MI)
    catA = np.concatenate([f(a).reshape(DEPTH, -1) for a in (w_gate, w_up, ple_gate_down, ple_gate_up)], axis=1).reshape(DEPTH, 11, 8, MI)
    wcat = np.concatenate([catB, catA], axis=1)
    del catA, catB
    in_maps = []
    for c in range(8):
        b, j = c // 2, c % 2
        xc = x[b].reshape(32, 128, D)[j::2].reshape(NTOK, D)
        pc = p[:, b].reshape(DEPTH, 32, 128, PLE)[:, j::2].reshape(DEPTH, NTOK, PLE)
        pTc = np.ascontiguousarray(pc.transpose(0, 2, 1)).reshape(DEPTH * PLE, NTOK)
        m = dict(small)
        m["wsh"] = np.ascontiguousarray(wcat[:, :, c, :]).reshape(DEPTH * 23 * 512, 2048)
        m["x"] = np.ascontiguousarray(xc)
        m["pT"] = pTc
        m.update(_consts(j))
        in_maps.append(m)
    res = run_bass_kernel_spmd(nc, in_maps, core_ids=list(range(8)))
    outp = np.empty((B, SEQ, D), np.float32)
    for c in range(8):
        b, j = c // 2, c % 2
        outp[b].reshape(32, 128, D)[j::2] = np.asarray(res.results[c]["out"]).reshape(16, 128, D)
    return outp
```

```python
import numpy as np
from contextlib import ExitStack
import concourse.bass as bass
import concourse.mybir as mybir
from concourse.bass_utils import run_bass_kernel_spmd

F32 = mybir.dt.float32
BF16 = mybir.dt.bfloat16
AF = mybir.ActivationFunctionType
ALU = mybir.AluOpType

DEPTH = 4
D = 4096
SEQ = 4096
NTOK = 2048
NBLK = 16
D_FF = 11008
IN_W = 9216
PLE = 256
RMS_EPS = 1e-6
LN_EPS = 1e-5
ATT_SCALE = 128 ** -0.5

ENGINES = ("pe", "act", "dve", "pool", "sp")
SEM_ROT = 16000
KSTOP = 99
KDBG = 0


class Op:
    __slots__ = ("eng", "fn", "deps", "is_dma", "dsem", "dval", "sig", "sigidx", "idx", "dinc", "nobar")


class Prog:
    def __init__(self, nc):
        self.nc = nc
        self.ops = {e: [] for e in ENGINES}
        self.last_w = {}
        self.readers = {}
        self.dma_tot = {}
        self.bar = {e: [] for e in ENGINES}
        self.out_dmas = []
        self.last_op = {}

    def _mk(self, eng, fn, reads, writes, extra, isdma=False, nobar=False):
        op = Op()
        op.eng = eng
        op.fn = fn
        op.is_dma = False
        op.sig = False
        op.dsem = None
        op.dval = 0
        op.dinc = 16
        op.nobar = nobar
        deps = []
        for r in reads:
            w = self.last_w.get(r)
            if w is not None:
                deps.append(w)
        for r in writes:
            w = self.last_w.get(r)
            if w is not None:
                deps.append(w)
            rs = self.readers.get(r)
            if rs:
                deps.extend(rs.values())
        deps.extend(extra)
        if not nobar and self.bar[eng]:
            deps.extend(self.bar[eng])
            self.bar[eng] = []
        op.deps = deps
        op.idx = len(self.ops[eng])
        for r in reads:
            rd = self.readers.get(r)
            if rd is None:
                rd = self.readers[r] = {}
            rd[eng if not isdma else (eng, op.idx)] = op
        for r in writes:
            self.last_w[r] = op
            self.readers[r] = {}
        self.ops[eng].append(op)
        if not isdma:
            self.last_op[eng] = op
        return op

    def op(self, eng, fn, reads=(), writes=(), extra=()):
        return self._mk(eng, fn, reads, writes, extra)

    def dma(self, eng, out, in_, semkey, reads=(), writes=(), extra=(), fn=None, inc=16, nobar=False, kw=None):
        if fn is None:
            kw = kw or {}

            def fn(e):
                return e.dma_start(out=out, in_=in_, **kw)
        op = self._mk(eng, fn, reads, writes, extra, isdma=True, nobar=nobar)
        op.is_dma = True
        op.dsem = semkey
        op.dinc = inc
        tot = self.dma_tot.get(semkey, 0) + inc
        self.dma_tot[semkey] = tot
        op.dval = tot
        if not nobar:
            self.out_dmas.append(op)
        return op

    def barrier(self):
        deps = list(self.last_op.values()) + self.out_dmas
        self.out_dmas = []
        for e in ENGINES:
            self.bar[e] = list(deps)

    def plan(self):
        for e in ENGINES:
            for op in self.ops[e]:
                for d in op.deps:
                    if not d.is_dma:
                        if d.eng == "pe" and op.eng == "pe" and not op.is_dma:
                            continue
                        d.sig = True
        nsig = {}
        for e in ENGINES:
            c = 0
            for op in self.ops[e]:
                if op.is_dma:
                    continue
                if op.sig:
                    c += 1
                    op.sigidx = c
            nsig[e] = c
        names = []
        for e in ENGINES:
            for r in range((nsig[e] + SEM_ROT - 1) // SEM_ROT):
                names.append("p_%s_%d" % (e, r))
        for k in self.dma_tot:
            names.append("d_" + str(k))
        return names

    def emit(self, block, sem_objs):
        def make_stream(e):
            ops = self.ops[e]

            def body(eng):
                waited = {}
                for op in ops:
                    for d in op.deps:
                        if d.is_dma:
                            sname = "d_" + str(d.dsem)
                            val = d.dval
                        else:
                            if d.eng == "pe" and e == "pe" and not op.is_dma:
                                continue
                            k = d.sigidx
                            sname = "p_%s_%d" % (d.eng, (k - 1) // SEM_ROT)
                            val = (k - 1) % SEM_ROT + 1
                        if waited.get(sname, 0) >= val:
                            continue
                        waited[sname] = val
                        eng.wait_ge(sem_objs[sname], val)
                    ins = op.fn(eng)
                    if op.is_dma:
                        ins.then_inc(sem_objs["d_" + str(op.dsem)], op.dinc)
                    elif op.sig:
                        k = op.sigidx
                        ins.then_inc(sem_objs["p_%s_%d" % (e, (k - 1) // SEM_ROT)], 1)
            return body

        block.tensor(make_stream("pe"))
        block.scalar(make_stream("act"))
        block.vector(make_stream("dve"))
        block.gpsimd(make_stream("pool"))
        block.sync(make_stream("sp"))


def build_program(depth=DEPTH):
    nc = bass.Bass("TRN2", target_bir_lowering=False)
    P = Prog(nc)
    uid = [0]

    def din(name, shape, dt=F32):
        return nc.dram_tensor(name, shape, dt, kind="ExternalInput").ap()

    def dscr(name, shape, dt):
        if KDBG and name in ("hbuf", "sguU", "sguV", "qT", "yT", "hidT", "g1T"):
            return nc.dram_tensor(name, shape, dt, kind="ExternalOutput")
        return nc.dram_tensor(name, shape, dt)

    x = din("x", [NTOK, D])
    pT = din("pT", [DEPTH * PLE, NTOK])
    MI = 1 << 20
    NCH = 23
    WOFF = {}
    GROUPS = (("B", 12, (("w_in", (D, IN_W)), ("w_out", (D, D)), ("w_down", (D_FF, D)), ("ple_proj", (PLE, D)))),
              ("A", 11, (("w_gate", (D, D_FF)), ("w_up", (D, D_FF)), ("ple_gate_down", (D, PLE)), ("ple_gate_up", (PLE, D)))))
    CH_BUF = []
    for gname, nch, lst in GROUPS:
        off = 0
        for nm, (R_, C_) in lst:
            WOFF[nm] = (gname, off, R_, C_)
            off += R_ * C_
        assert off == nch * 8 * MI
        CH_BUF += [(gname, i) for i in range(nch)]
    wsh = din("wsh", [DEPTH * NCH * 512, 2048])
    wlocs = {}
    wbig = {}
    for l in range(depth):
        for gname, nch, lst in GROUPS:
            wbig[l, gname] = dscr("wbig%s%d" % (gname, l), [nch * 4096, 2048], BF16)
        for j in range(NCH):
            wlocs[l, j] = dscr("wloc%d_%d" % (l, j), [512, 2048], BF16)
    WCH = {}
    for nm, (gname, o_, R_, C_) in WOFF.items():
        base = 0 if gname == "B" else 12
        WCH[nm] = list(range(base + o_ // (8 * MI), base + (o_ + R_ * C_ - 1) // (8 * MI) + 1))
    wfull = {}
    for l in range(depth):
        for nm, (gname, o_, R_, C_) in WOFF.items():
            flat = wbig[l, gname].ap().rearrange("a b -> (a b)")
            wfull[nm, l] = flat[o_:o_ + R_ * C_].rearrange("(r c) -> r c", c=C_)
    pool_w = din("pool_w", [DEPTH * 4 * 256, 256])
    pool_scale = din("pool_scale", [DEPTH, 1024])
    sgu_nw = din("sgu_norm_w", [DEPTH, 1024])
    sgu_w = din("sgu_w", [DEPTH * 8 * 128, 128])
    sgu_b = din("sgu_b", [DEPTH, 1024])
    nmix = din("norm_mix_w", [DEPTH, D])
    nffn = din("norm_ffn_w", [DEPTH, D])
    nple = din("norm_ple_w", [DEPTH, D])
    nfin = din("final_norm_w", [1, D])
    c_ident = din("c_ident", [128, 128])
    c_tril = din("c_tril", [128, 128])
    c_m2 = din("c_m2", [128, 256])
    c_sel = din("c_sel", [128, 2])
    c_invc = din("c_invc", [128, 4 * 128])
    out = nc.dram_tensor("out", [NTOK, D], F32, kind="ExternalOutput").ap()

    hbuf = dscr("hbuf", [NTOK, D], F32).ap()
    zpool_loc = dscr("zpool_loc", [1024, NTOK], F32)
    zh_loc = dscr("zh_loc", [1024, 1024], BF16)
    zh_all = dscr("zh_all", [2048, 1024], BF16)
    sguU = dscr("sguU", [1024, NTOK], F32).ap()
    sguV = dscr("sguV", [NTOK, 1024], F32).ap()
    qT = dscr("qT", [2048, NTOK], BF16).ap()
    kT_loc = [dscr("kT_loc%d" % q, [512, NTOK], BF16) for q in range(4)]
    kT_all = [dscr("kT_all%d" % q, [1024, NTOK], BF16) for q in range(4)]
    v_loc = [dscr("v_loc%d" % q, [512, 2048], BF16) for q in range(4)]
    v_all = [dscr("v_all%d" % q, [1024, 2048], BF16) for q in range(4)]
    yT = dscr("yT", [D, NTOK], BF16).ap()
    hidT = dscr("hidT", [D_FF, NTOK], BF16).ap()
    g1T = dscr("g1T", [PLE, NTOK], BF16).ap()

    A = nc.alloc_sbuf_tensor
    NSLAB = 6
    ring = [A("ring%d" % i, [128, 8, 512], BF16).ap() for i in range(NSLAB)]
    ident = A("ident", [128, 128], BF16).ap()
    tril = A("tril", [128, 128], F32).ap()
    m2 = A("m2", [128, 256], F32).ap()
    sel = A("sel", [128, 2], F32).ap()
    invc = A("invc", [128, 4, 128], F32).ap()
    epsr = A("epsr", [128, 1], F32).ap()
    epsl = A("epsl", [128, 1], F32).ap()
    oneb = A("oneb", [128, 1], F32).ap()
    pscol = A("pscol", [128, DEPTH, 8], F32).ap()
    banks = [nc.alloc_psum_tensor("bank%d" % i, [128, 512], F32).ap() for i in range(8)]

    def BK(i):
        return ("bank", i)

    P.op("dve", lambda e: e.memset(epsr, RMS_EPS), writes=["epsr"])
    P.op("dve", lambda e: e.memset(epsl, LN_EPS), writes=["epsl"])
    P.op("dve", lambda e: e.memset(oneb, 1.0), writes=["oneb"])
    P.dma("pool", ident, c_ident, "c", writes=["ident"])
    P.dma("sp", tril, c_tril, "c", writes=["tril"])
    P.dma("sp", m2, c_m2, "c", writes=["m2"])
    P.dma("sp", sel, c_sel, "c", writes=["sel"])
    P.dma("sp", invc.rearrange("p g t -> p (g t)"), c_invc, "c", writes=["invc"])
    for l in range(depth):
        P.dma("sp", pscol[:, l, :], pool_scale[l:l + 1, :].rearrange("o (c p) -> p (o c)", p=128), "c",
              writes=[("pscol", l)], kw=dict(allow_slow_non_contiguous=True))
    P.barrier()

    for l in range(depth):
        for j in range(NCH):
            P.dma("pool", wlocs[l, j].ap(), wsh[(l * NCH + j) * 512:(l * NCH + j + 1) * 512, :], "bc%d" % l,
                  writes=[("wloc", l, j)], nobar=True)
    all_wloc = [("wloc", l, j) for l in range(depth) for j in range(NCH)]

    def gather_weights(l):
        for j in range(NCH):
            P.dma("pool", None, None, "ccw", reads=[("wloc", l, j_) for j_ in range(NCH)], writes=[("wbig", l, j)], inc=1, nobar=True,
                  fn=lambda e, l=l, j=j: e.collective_compute(
                      "AllGather", ALU.bypass, replica_groups=[list(range(8))], ins=[wlocs[l, j].ap().opt()],
                      outs=[wbig[l, CH_BUF[j][0]].ap()[CH_BUF[j][1] * 4096:(CH_BUF[j][1] + 1) * 4096, :].opt()]))
    gather_weights(0)

    ring_state = {"i": 0}
    grp_state = {"i": 0}

    def sb(st, name, shape, dt):
        uid[0] += 1
        return st.enter_context(nc.sbuf_tensor("%s_%d" % (name, uid[0]), shape, dt)).ap()

    def copy_op(eng, o, i, reads, writes):
        if eng == "act":
            return P.op("act", lambda e: e.copy(out=o, in_=i), reads=reads, writes=writes)
        return P.op(eng, lambda e: e.tensor_copy(out=o, in_=i), reads=reads, writes=writes)

    def norm_T(st, src, row0, nb, nw_row, actT, akey):
        wB = sb(st, "wB", [128, D], F32)
        ht = [sb(st, "ht", [128, D], F32) for _ in range(2)]
        xn = [sb(st, "xn", [128, D], BF16) for _ in range(2)]
        ss = sb(st, "ss", [128, 4], F32)
        k_wB = ("wB", uid[0])
        P.dma("sp", wB, nw_row.partition_broadcast(128), "nw", writes=[k_wB])
        for b in range(nb):
            h = ht[b % 2]
            xb = xn[b % 2]
            hk = ("ht", b % 2)
            xk = ("xn", b % 2)
            P.dma("sp", h, src[row0 + b * 128: row0 + (b + 1) * 128, :], "ht%d" % (b % 2), writes=[hk])
            P.op("act", lambda e, h=h, xb=xb: e.activation(out=xb, in_=h, func=AF.Square, accum_out=ss[:, 0:1]),
                 reads=[hk], writes=[xk, "ss0"])
            P.op("act", lambda e: e.activation(out=ss[:, 1:2], in_=ss[:, 0:1], func=AF.Sqrt, bias=epsr[:, 0:1],
                                               scale=1.0 / D), reads=["ss0", "epsr"], writes=["ss1"])
            P.op("dve", lambda e: e.reciprocal(out=ss[:, 2:3], in_=ss[:, 1:2]), reads=["ss1"], writes=["ss2"])
            P.op("dve", lambda e, h=h, xb=xb: e.scalar_tensor_tensor(out=xb, in0=h, scalar=ss[:, 2:3], in1=wB,
                                                                      op0=ALU.mult, op1=ALU.mult),
                 reads=[hk, "ss2", k_wB], writes=[xk])
            for g in range(8):
                bk = g % 2
                pt = banks[bk].bitcast(BF16)[:, 0:512].rearrange("p (c t) -> p c t", c=4)
                for c in range(4):
                    cc = g * 4 + c
                    P.op("pe", lambda e, pt=pt, c=c, cc=cc, xb=xb: e.transpose(pt[:, c, :], xb[:, cc * 128:(cc + 1) * 128], ident),
                         reads=[xk, "ident"], writes=[BK(bk)])
                dst = actT[:, g * 4:(g + 1) * 4, b * 128:(b + 1) * 128]
                copy_op("act" if g % 2 == 0 else "dve", dst, pt, [BK(bk)], [(akey, b, g)])

    def linear(actT, akey, KC, TT, panels):
        nslab = (KC + 7) // 8
        ngrp = TT // 512
        assert ngrp == 1 or nslab <= NSLAB - 1
        for pn in panels:
            mode = pn["mode"]
            ncols = pn["ncols"]
            slabs = {}

            def load_slab(s, pn=pn, slabs=slabs):
                r = ring_state["i"] % NSLAB
                ring_state["i"] += 1
                rg = ring[r]
                kcs = min(8, KC - s * 8)
                for hf in range(0, kcs, 4):
                    nk = min(4, kcs - hf)
                    for si, (src, coff) in enumerate(pn["srcs"]):
                        w = src.shape[1]
                        k0 = (s * 8 + hf) * 128
                        sa = src[k0:k0 + nk * 128, :].rearrange("(kc p) n -> p kc n", p=128)
                        P.dma("pool", rg[:, hf:hf + nk, coff:coff + w], sa, "rg%d_%d" % (r, hf // 4),
                              reads=[("wbig", pn["wkeys"][si][2], j_) for j_ in WCH[pn["wkeys"][si][1]]], writes=[("ring", r, hf // 4, si)], nobar=True)
                slabs[s] = (r, rg, kcs)

            for g in range(ngrp):
                bk0 = (grp_state["i"] % 2) * 4
                grp_state["i"] += 1
                nj = 4 if mode == "tok" else ncols // 128
                for s in range(nslab):
                    if g == 0:
                        load_slab(s)
                    r, rg, kcs = slabs[s]
                    for kc in range(kcs):
                        kk = s * 8 + kc
                        rk = [("ring", r, kc // 4, 0), ("ring", r, kc // 4, 1)]
                        for j in range(nj):
                            if mode == "tok":
                                lhsT = actT[:, kk, g * 512 + j * 128: g * 512 + (j + 1) * 128]
                                rhs = rg[:, kc, 0:ncols]
                                o = banks[bk0 + j][:, 0:ncols]
                            else:
                                lhsT = rg[:, kc, j * 128:(j + 1) * 128]
                                rhs = actT[:, kk, g * 512:(g + 1) * 512]
                                o = banks[bk0 + j]
                            P.op("pe", lambda e, o=o, l=lhsT, rh=rhs, st_=(kk == 0), sp_=(kk == KC - 1):
                                 e.matmul(o, l, rh, start=st_, stop=sp_), reads=rk, writes=[BK(bk0 + j)])
                pn["epi"](pn, g, bk0)

    def make_ev(st, n, shape, dt, name):
        tiles = [sb(st, name, shape, dt) for _ in range(n)]
        cnt = [0]
        base = uid[0]

        def nxt():
            i = cnt[0] % n
            cnt[0] += 1
            return tiles[i], (name, base, i)
        return nxt

    def epi_store_feat(nxt, dst, row0, tok0, dt_eng=("act", "dve")):
        def epi(pn, g, bk0):
            for j in range(pn["ncols"] // 128):
                t, k = nxt()
                copy_op(dt_eng[j % 2], t, banks[bk0 + j], [BK(bk0 + j)], [k])
                rr = row0 + pn["n0"] + j * 128
                if callable(dst):
                    d_, rr = dst(rr)
                else:
                    d_ = dst
                P.dma("sp", d_[rr: rr + 128, tok0 + g * 512: tok0 + (g + 1) * 512], t, "st_%s_%d" % (k[0], k[2]), reads=[k])
        return epi

    def epi_store_tok(nxt, dst, col0, tok0):
        def epi(pn, g, bk0):
            nco = pn["ncols"]
            for j in range(4):
                t, k = nxt()
                copy_op(("act", "dve")[j % 2], t[:, 0:nco], banks[bk0 + j][:, 0:nco], [BK(bk0 + j)], [k])
                r0 = tok0 + g * 512 + j * 128
                if callable(dst):
                    d_, r0 = dst(r0)
                else:
                    d_ = dst
                P.dma("sp", d_[r0:r0 + 128, col0 + pn["n0"]: col0 + pn["n0"] + nco], t[:, 0:nco],
                      "st_%s_%d" % (k[0], k[2]), reads=[k])
        return epi

    def epi_resid(nxt_h, hsrc, tok0):
        def epi(pn, g, bk0):
            nco = pn["ncols"]
            for j in range(4):
                t, k = nxt_h()
                r0 = tok0 + g * 512 + j * 128
                c0 = pn["n0"]
                P.dma("sp", t[:, 0:nco], hsrc[r0:r0 + 128, c0:c0 + nco], "ld_%s_%d" % (k[0], k[2]), writes=[k])
                P.op("dve", lambda e, t=t, b=banks[bk0 + j], nco=nco: e.tensor_tensor(out=t[:, 0:nco], in0=b[:, 0:nco],
                                                                                      in1=t[:, 0:nco], op=ALU.add),
                     reads=[BK(bk0 + j), k], writes=[k])
                P.dma("sp", hbuf[r0:r0 + 128, c0:c0 + nco], t[:, 0:nco], "st_%s_%d" % (k[0], k[2]), reads=[k])
        return epi

    for l in range(depth):
        hsrc = x if l == 0 else hbuf

        for ps in range(2 if KSTOP >= 0 else 0):
            tok0 = ps * 1024
            with ExitStack() as st:
                actT = sb(st, "actT", [128, 32, 1024], BF16)
                with ExitStack() as st2:
                    norm_T(st2, hsrc, tok0, 8, nmix[l:l + 1, :], actT, "actT")
                    P.barrier()
                ev32 = make_ev(st, 4, [128, 512], F32, "ev32")
                ev16 = make_ev(st, 4, [128, 512], BF16, "ev16")
                W = wfull["w_in", l]
                panels = []

                def add(c0, n, mode, epi, n0base):
                    for i in range(n):
                        panels.append(dict(srcs=[(W[:, c0 + i * 512: c0 + (i + 1) * 512], 0)], ncols=512, mode=mode,
                                           epi=epi, n0=n0base + i * 512, wkeys=[("wfull", "w_in", l)]))
                add(0, 2, "feat", epi_store_feat(ev32, zpool_loc.ap(), 0, tok0), 0)
                add(1024, 2, "feat", epi_store_feat(ev32, sguU, 0, tok0), 0)
                add(2048, 2, "tok", epi_store_tok(ev32, sguV, 0, tok0), 0)
                add(3072, 4, "feat", epi_store_feat(ev16, qT, 0, tok0), 0)
                add(5120, 4, "feat", epi_store_feat(ev16, lambda r: (kT_loc[r // 512].ap(), r % 512), 0, tok0), 0)
                add(7168, 4, "tok", epi_store_tok(ev16, lambda r: (v_loc[r // 512].ap(), r % 512), 0, tok0), 0)
                linear(actT, "actT", 32, 1024, panels)
                P.barrier()

        rgp = [[0, 1], [2, 3], [4, 5], [6, 7]]
        zhl = zh_loc.ap().bitcast(F32)
        for c8 in range(8):
            P.dma("sp", zhl[c8 * 128:(c8 + 1) * 128, :].rearrange("p (m t) -> p m t", m=NBLK),
                  zpool_loc.ap()[c8 * 128:(c8 + 1) * 128, :].rearrange("p (m t) -> p m t", m=NBLK)[:, :, 96:128],
                  "zh", writes=[("zhl", c8)])
        P.barrier()
        pairs = [("zp", zh_loc, zh_all)] + [("kT%d" % q, kT_loc[q], kT_all[q]) for q in range(4)] + \
                [("v%d" % q, v_loc[q], v_all[q]) for q in range(4)]
        cc_ops = []
        for nm, a, b_ in pairs:
            cc_ops.append(P.dma("pool", None, None, "cc_x", writes=[("cc", nm)], inc=1, nobar=False,
                                fn=lambda e, a=a, b_=b_: e.collective_compute("AllGather", ALU.bypass, replica_groups=rgp,
                                                                               ins=[a.ap().opt()], outs=[b_.ap().opt()])))
        del P.out_dmas[-len(pairs):]
        if l + 1 < depth:
            gather_weights(l + 1)

        if KSTOP < 2:
            continue
        with ExitStack() as st:
            wraw = sb(st, "wraw", [128, 8, 128], F32)
            wm = sb(st, "wm", [128, 8, 128], BF16)
            wcT = sb(st, "wcT", [128, 8, 128], BF16)
            bB = sb(st, "bB", [128, 1024], F32)
            nwB = sb(st, "nwB", [128, 1024], F32)
            P.dma("sp", wraw, sgu_w[l * 1024:(l + 1) * 1024, :].rearrange("(h t) s -> t h s", h=8), "sg",
                  writes=["wraw"])
            P.dma("sp", bB, sgu_b[l:l + 1, :].partition_broadcast(128), "sg", writes=["bB"])
            P.dma("sp", nwB, sgu_nw[l:l + 1, :].partition_broadcast(128), "sg", writes=["nwB"])
            for hh in range(8):
                P.op("dve", lambda e, hh=hh: e.tensor_tensor(out=wm[:, hh, :], in0=wraw[:, hh, :], in1=tril, op=ALU.mult),
                     reads=["wraw", "tril", "bB", "nwB"], writes=[("wm", hh)])
            for hg in range(2):
                pt = banks[hg].bitcast(BF16)[:, 0:512].rearrange("p (c t) -> p c t", c=4)
                for c in range(4):
                    hh = hg * 4 + c
                    P.op("pe", lambda e, pt=pt, c=c, hh=hh: e.transpose(pt[:, c, :], wm[:, hh, :], ident),
                         reads=[("wm", hh), "ident"], writes=[BK(hg)])
                copy_op("act", wcT[:, hg * 4:(hg + 1) * 4, :], pt, [BK(hg)], [("wcT", hg)])
            NB2 = 2
            vr = [sb(st, "vr", [128, 1024], F32) for _ in range(NB2)]
            ur = [sb(st, "ur", [128, 1024], F32) for _ in range(NB2)]
            t1 = [sb(st, "t1", [128, 1024], F32) for _ in range(NB2)]
            t2 = [sb(st, "t2", [128, 1024], F32) for _ in range(NB2)]
            vln = [sb(st, "vln", [128, 1024], BF16) for _ in range(NB2)]
            yo = [sb(st, "yo", [128, 1024], BF16) for _ in range(NB2)]
            stt = [sb(st, "stt", [128, 8], F32) for _ in range(NB2)]

            def gelu(xt, xk, ta, tak, tb, tbk, eng2="dve"):
                P.op("dve", lambda e: e.tensor_tensor(out=ta, in0=xt, in1=xt, op=ALU.mult), reads=[xk], writes=[tak])
                P.op("dve", lambda e: e.tensor_scalar(out=ta, in0=ta, scalar1=0.044715, scalar2=1.0, op0=ALU.mult,
                                                       op1=ALU.add), reads=[tak], writes=[tak])
                P.op("dve", lambda e: e.tensor_tensor(out=ta, in0=ta, in1=xt, op=ALU.mult), reads=[tak, xk], writes=[tak])
                P.op("act", lambda e: e.activation(out=tb, in_=ta, func=AF.Sigmoid, scale=1.5957691216057308),
                     reads=[tak], writes=[tbk])
                P.op("dve", lambda e: e.tensor_tensor(out=tb, in0=tb, in1=xt, op=ALU.mult), reads=[tbk, xk], writes=[tbk])

            for m in range(NBLK):
                i = m % NB2
                V, U, T1, T2, VL, YO, S = vr[i], ur[i], t1[i], t2[i], vln[i], yo[i], stt[i]
                kV, kU, kT1, kT2, kVL, kYO, kS = ("vr", i), ("ur", i), ("t1", i), ("t2", i), ("vln", i), ("yo", i), ("stt", i)
                P.dma("sp", V, sguV[m * 128:(m + 1) * 128, :], "sgl%d" % i, writes=[kV])
                P.dma("sp", U.rearrange("p (h t) -> p h t", h=8),
                      sguU[:, m * 128:(m + 1) * 128].rearrange("(h d) t -> d h t", h=8), "sgl%d" % i, writes=[kU])
                P.op("dve", lambda e, S=S: e.memset(S[:, 0:2], 0.0), reads=[kV, kU], writes=[(kS, 0), (kS, 1)])
                gelu(V, kV, T1, kT1, T2, kT2)
                P.op("act", lambda e, T1=T1, T2=T2, S=S: e.activation(out=T1, in_=T2, func=AF.Identity, accum_out=S[:, 0:1]),
                     reads=[kT2], writes=[kT1, (kS, 0)])
                P.op("act", lambda e, T1=T1, T2=T2, S=S: e.activation(out=T1, in_=T2, func=AF.Square, accum_out=S[:, 1:2]),
                     reads=[kT2], writes=[kT1, (kS, 1)])
                P.op("dve", lambda e, S=S: e.tensor_scalar(out=S[:, 2:3], in0=S[:, 0:1], scalar1=1.0 / 1024, scalar2=None,
                                                            op0=ALU.mult), reads=[(kS, 0)], writes=[(kS, 2)])
                P.op("dve", lambda e, S=S: e.tensor_tensor(out=S[:, 3:4], in0=S[:, 2:3], in1=S[:, 2:3], op=ALU.mult),
                     reads=[(kS, 2)], writes=[(kS, 3)])
                P.op("dve", lambda e, S=S: e.scalar_tensor_tensor(out=S[:, 4:5], in0=S[:, 1:2], scalar=1.0 / 1024,
                                                                   in1=S[:, 3:4], op0=ALU.mult, op1=ALU.subtract),
                     reads=[(kS, 1), (kS, 3)], writes=[(kS, 4)])
                P.op("act", lambda e, S=S: e.activation(out=S[:, 5:6], in_=S[:, 4:5], func=AF.Sqrt, bias=epsl[:, 0:1], scale=1.0),
                     reads=[(kS, 4), "epsl"], writes=[(kS, 5)])
                P.op("dve", lambda e, S=S: e.reciprocal(out=S[:, 6:7], in_=S[:, 5:6]), reads=[(kS, 5)], writes=[(kS, 6)])
                P.op("dve", lambda e, S=S: e.scalar_tensor_tensor(out=S[:, 7:8], in0=S[:, 2:3], scalar=-1.0, in1=S[:, 6:7],
                                                                   op0=ALU.mult, op1=ALU.mult),
                     reads=[(kS, 2), (kS, 6)], writes=[(kS, 7)])
                P.op("dve", lambda e, T1=T1, T2=T2, S=S: e.tensor_scalar(out=T1, in0=T2, scalar1=S[:, 6:7], scalar2=S[:, 7:8],
                                                                          op0=ALU.mult, op1=ALU.add),
                     reads=[kT2, (kS, 6), (kS, 7)], writes=[kT1])
                P.op("dve", lambda e, T1=T1, VL=VL: e.tensor_tensor(out=VL, in0=T1, in1=nwB, op=ALU.mult),
                     reads=[kT1, "nwB"], writes=[kVL])
                for hh in range(8):
                    bk = 2 + hh // 4
                    P.op("pe", lambda e, hh=hh, bk=bk, VL=VL: e.matmul(banks[bk][:, (hh % 4) * 128:(hh % 4 + 1) * 128],
                                                                        VL[:, hh * 128:(hh + 1) * 128], wcT[:, hh, :],
                                                                        start=True, stop=True),
                         reads=[kVL, ("wcT", hh // 4)], writes=[BK(bk)])
                gelu(U, kU, T1, kT1, T2, kT2)
                for hf in range(2):
                    P.op("dve", lambda e, hf=hf, T1=T1: e.tensor_tensor(out=T1[:, hf * 512:(hf + 1) * 512], in0=banks[2 + hf],
                                                                        in1=bB[:, hf * 512:(hf + 1) * 512], op=ALU.add),
                         reads=[BK(2 + hf), "bB"], writes=[kT1])
                P.op("dve", lambda e, T1=T1, T2=T2, YO=YO: e.tensor_tensor(out=YO, in0=T1, in1=T2, op=ALU.mult),
                     reads=[kT1, kT2], writes=[kYO])
                P.dma("sp", yT[1024:2048, m * 128:(m + 1) * 128].rearrange("(h d) t -> d h t", h=8),
                      YO.rearrange("p (h t) -> p h t", h=8), "sgy%d" % i, reads=[kYO])
            P.barrier()

        if KSTOP < 3:
            continue
        with ExitStack() as st:
            pw = sb(st, "pw", [128, 8, 256], BF16)
            for g in range(4):
                P.dma("pool", pw[:, g * 2:(g + 1) * 2, :],
                      pool_w[(l * 4 + g) * 256:(l * 4 + g + 1) * 256, :].rearrange("(cc p) d -> p cc d", p=128),
                      "pw", writes=[("pw", g)])
            pdT = sb(st, "pdT", [128, 8, NTOK], BF16)
            At = [sb(st, "pA", [128, NBLK, 144], F32) for _ in range(2)]
            Bt = [sb(st, "pB", [128, NBLK, 144], F32) for _ in range(2)]
            Ct = [sb(st, "pC", [128, NBLK, 144], F32) for _ in range(2)]
            cA = [sb(st, "cA", [128, NBLK, 16], F32) for _ in range(2)]
            cB = [sb(st, "cB", [128, NBLK, 16], F32) for _ in range(2)]
            for i in range(2):
                P.op("dve", lambda e, i=i: e.memset(cB[i], 0.0), writes=[("cB", i)])
            zl = zpool_loc.ap()
            za = zh_all.ap().bitcast(F32)
            for c in range(8):
                i = c % 2
                g = c // 2
                w = 2 << g
                a_, b_, c_ = At[i], Bt[i], Ct[i]
                kA, kB, kC, kcA, kcB = ("pA", i), ("pB", i), ("pC", i), ("cA", i), ("cB", i)
                P.dma("sp", a_[:, :, 16:144], zl[c * 128:(c + 1) * 128, :].rearrange("p (m t) -> p m t", m=NBLK),
                      "pl%d" % i, writes=[(kA, "main")])
                P.dma("sp", cA[i], za[c * 128:(c + 1) * 128, :].rearrange("p (m t) -> p m t", m=NBLK)[:, :, 16:32],
                      "pl%d" % i, writes=[kcA], extra=cc_ops)
                P.dma("sp", cB[i][:, 1:NBLK, :],
                      za[1024 + c * 128:1024 + (c + 1) * 128, :].rearrange("p (m t) -> p m t", m=NBLK)[:, 0:NBLK - 1, 16:32],
                      "pl%d" % i, writes=[kcB], extra=cc_ops)
                P.op("dve", lambda e, i=i: e.tensor_scalar(out=cA[i], in0=cA[i], scalar1=sel[:, 0:1], scalar2=None, op0=ALU.mult),
                     reads=["sel", kcA, kcB, (kA, "main")], writes=[kcA])
                P.op("dve", lambda e, i=i, a_=a_: e.scalar_tensor_tensor(out=a_[:, :, 0:16], in0=cB[i], scalar=sel[:, 1:2],
                                                                         in1=cA[i], op0=ALU.mult, op1=ALU.add),
                     reads=["sel", kcA, kcB], writes=[(kA, "halo")])
                cur, kcur, lo, span = a_, None, 0, 1
                bufs = [(b_, kB), (c_, kC)]
                step = 0
                rdk = [(kA, "main"), (kA, "halo")]
                while span < w:
                    nb_, nk_ = bufs[step % 2]
                    nlo = lo + span
                    P.op("dve", lambda e, cur=cur, nb_=nb_, nlo=nlo, span=span: e.tensor_tensor(
                        out=nb_[:, :, nlo:144], in0=cur[:, :, nlo:144], in1=cur[:, :, nlo - span:144 - span], op=ALU.add),
                        reads=rdk, writes=[nk_])
                    cur, lo, span = nb_, nlo, span * 2
                    rdk = [nk_]
                    step += 1
                P.op("dve", lambda e, cur=cur, a_=a_, c=c, w=w: e.scalar_tensor_tensor(
                    out=pdT[:, c, :].rearrange("p (m t) -> p m t", m=NBLK), in0=cur[:, :, 16:144], scalar=1.0 / w,
                    in1=a_[:, :, 16:144], op0=ALU.mult, op1=ALU.subtract),
                    reads=rdk + [(kA, "main")], writes=[("pdT", c)])
                fx, kfx = bufs[step % 2]
                P.op("dve", lambda e, cur=cur, fx=fx, g=g: e.tensor_tensor(out=fx[:, 0, 16:144], in0=cur[:, 0, 16:144],
                                                                           in1=invc[:, g, :], op=ALU.mult),
                     reads=rdk + ["invc"], writes=[kfx])
                P.op("dve", lambda e, fx=fx, a_=a_, c=c: e.tensor_tensor(out=pdT[:, c, 0:128], in0=fx[:, 0, 16:144],
                                                                         in1=a_[:, 0, 16:144], op=ALU.subtract),
                     reads=[kfx, (kA, "main"), ("pdT", c)], writes=[("pdT", c)])
            evp = make_ev(st, 4, [128, 512], BF16, "evp")
            gi = 0
            for g in range(4):
                for dc in range(2):
                    for tt in range(4):
                        bk = gi % 4
                        gi += 1
                        for cc in range(2):
                            P.op("pe", lambda e, g=g, dc=dc, tt=tt, cc=cc, bk=bk: e.matmul(
                                banks[bk], pw[:, g * 2 + cc, dc * 128:(dc + 1) * 128],
                                pdT[:, g * 2 + cc, tt * 512:(tt + 1) * 512], start=(cc == 0), stop=(cc == 1)),
                                reads=[("pw", g), ("pdT", g * 2 + cc)], writes=[BK(bk)])
                        t, k = evp()
                        P.op("act", lambda e, t=t, bk=bk, g=g, dc=dc, l=l: e.activation(out=t, in_=banks[bk], func=AF.Copy,
                                                                                  scale=pscol[:, l, g * 2 + dc: g * 2 + dc + 1]),
                             reads=[BK(bk), ("pscol", l)], writes=[k])
                        r0 = (g * 2 + dc) * 128
                        P.dma("sp", yT[r0:r0 + 128, tt * 512:(tt + 1) * 512], t, "st_%s_%d" % (k[0], k[2]), reads=[k])
            P.barrier()

        if KSTOP < 4:
            continue
        with ExitStack() as st:
            KT = sb(st, "KT", [128, 32, 128], BF16)
            VT = sb(st, "VT", [128, 32, 128], BF16)
            QT = sb(st, "QT", [128, NTOK], BF16)
            E_ = [sb(st, "E", [128, 4096], F32) for _ in range(2)]
            S_ = [sb(st, "S", [128, 4096], F32) for _ in range(2)]
            Pf = [sb(st, "Pf", [128, 4096], F32) for _ in range(2)]
            Ab = [sb(st, "Ab", [128, 4096], BF16) for _ in range(2)]
            ATt = [sb(st, "ATt", [128, 32, 128], BF16) for _ in range(2)]
            nt_ = [sb(st, "nt", [128, 2], F32) for _ in range(2)]
            oev = make_ev(st, 2, [128, 512], BF16, "oev")
            qi = 0
            for hh in range(16):
                ka = kT_all[hh // 4].ap()
                for r in range(2):
                    P.dma("sp", KT.rearrange("p (m r) t -> p m r t", r=2)[:, :, r, :],
                          ka[r * 512 + (hh % 4) * 128: r * 512 + (hh % 4 + 1) * 128, :].rearrange("p (m t) -> p m t", m=NBLK),
                          "kt", writes=[("KT", r)], extra=cc_ops)
                    for q in range(4):
                        va = v_all[q].ap()
                        P.dma("sp", VT.rearrange("p (m r) d -> p m r d", r=2)[:, q * 4:(q + 1) * 4, r, :],
                              va[r * 512:(r + 1) * 512, hh * 128:(hh + 1) * 128].rearrange("(m s) d -> s m d", s=128),
                              "vt", writes=[("VT", r, q)], extra=cc_ops)
                P.dma("sp", QT, qT[hh * 128:(hh + 1) * 128, :], "qt", writes=["QT"])
                for m in range(NBLK):
                    i = qi % 2
                    qi += 1
                    E, S, PF, AB, AT, NT = E_[i], S_[i], Pf[i], Ab[i], ATt[i], nt_[i]
                    kE, kS, kPF, kAB, kAT, kNT = ("E", i), ("S", i), ("Pf", i), ("Ab", i), ("ATt", i), ("nt", i)
                    nk = 2 * m + 2
                    ncol = nk * 128
                    ntile = (ncol + 511) // 512
                    for tl in range(ntile):
                        c0 = tl * 512
                        cw = min(512, ncol - c0)
                        bk = tl % 4
                        P.op("pe", lambda e, bk=bk, cw=cw, c0=c0, m=m: e.matmul(
                            banks[bk][:, 0:cw], QT[:, m * 128:(m + 1) * 128],
                            KT.rearrange("p g t -> p (g t)")[:, c0:c0 + cw], start=True, stop=True),
                            reads=["QT", ("KT", 0), ("KT", 1)], writes=[BK(bk)])
                        P.op("act", lambda e, bk=bk, cw=cw, c0=c0, E=E: e.activation(out=E[:, c0:c0 + cw], in_=banks[bk][:, 0:cw],
                                                                                      func=AF.Exp, scale=ATT_SCALE),
                             reads=[BK(bk)], writes=[kE])
                    P.op("dve", lambda e, E=E, ncol=ncol: e.tensor_tensor(out=E[:, ncol - 256:ncol], in0=E[:, ncol - 256:ncol],
                                                                          in1=m2, op=ALU.mult), reads=[kE, "m2"], writes=[kE])
                    P.op("act", lambda e, E=E, S=S, ncol=ncol: e.activation(out=S[:, 0:ncol], in_=E[:, 0:ncol], func=AF.Ln,
                                                                             bias=oneb[:, 0:1], scale=1.0), reads=[kE, "oneb"], writes=[kS])
                    P.op("dve", lambda e, S=S, PF=PF, ncol=ncol: e.tensor_tensor_scan(out=PF[:, 0:ncol], data0=S[:, 0:ncol],
                                                                                       data1=S[:, 0:ncol], initial=0.0,
                                                                                       op0=ALU.add, op1=ALU.max),
                         reads=[kS], writes=[kPF])
                    P.op("dve", lambda e, PF=PF, NT=NT, ncol=ncol: e.tensor_scalar(out=NT[:, 0:1], in0=PF[:, ncol - 1:ncol],
                                                                                    scalar1=-1.0, scalar2=None, op0=ALU.mult),
                         reads=[kPF], writes=[kNT])
                    P.op("dve", lambda e, S=S, PF=PF, ncol=ncol: e.tensor_tensor(out=S[:, 0:ncol], in0=PF[:, 0:ncol],
                                                                                 in1=S[:, 0:ncol], op=ALU.subtract),
                         reads=[kPF, kS], writes=[kS])
                    P.op("act", lambda e, S=S, PF=PF, NT=NT, ncol=ncol: e.activation(out=PF[:, 0:ncol], in_=S[:, 0:ncol],
                                                                                      func=AF.Exp, bias=NT[:, 0:1], scale=1.0),
                         reads=[kS, kNT], writes=[kPF])
                    P.op("dve", lambda e, E=E, PF=PF, AB=AB, ncol=ncol: e.tensor_tensor(out=AB[:, 0:ncol], in0=E[:, 0:ncol],
                                                                                        in1=PF[:, 0:ncol], op=ALU.mult),
                         reads=[kE, kPF], writes=[kAB])
                    for g4 in range((nk + 3) // 4):
                        bk = 4 + g4 % 2
                        n4 = min(4, nk - g4 * 4)
                        pt = banks[bk].bitcast(BF16)[:, 0:512].rearrange("p (c t) -> p c t", c=4)
                        for c in range(n4):
                            G = g4 * 4 + c
                            P.op("pe", lambda e, pt=pt, c=c, G=G, AB=AB: e.transpose(pt[:, c, :], AB[:, G * 128:(G + 1) * 128], ident),
                                 reads=[kAB, "ident"], writes=[BK(bk)])
                        copy_op("act" if g4 % 2 == 0 else "dve", AT[:, g4 * 4:g4 * 4 + n4, :], pt[:, 0:n4, :], [BK(bk)], [kAT])
                    bk = 6 + (hh * 4 + m // 4) % 2
                    for G in range(nk):
                        P.op("pe", lambda e, bk=bk, G=G, AT=AT, m=m: e.matmul(
                            banks[bk][:, (m % 4) * 128:(m % 4 + 1) * 128], VT[:, G, :], AT[:, G, :],
                            start=(G == 0), stop=(G == nk - 1)),
                            reads=[kAT] + [("VT", r_, q_) for r_ in range(2) for q_ in range(4)], writes=[BK(bk)])
                    if m % 4 == 3:
                        t, k = oev()
                        copy_op("dve", t, banks[bk], [BK(bk)], [k])
                        r0 = 2048 + hh * 128
                        P.dma("sp", yT[r0:r0 + 128, (m - 3) * 128:(m + 1) * 128], t, "st_%s_%d" % (k[0], k[2]), reads=[k])
            P.barrier()

        if KSTOP < 5:
            continue
        for ps in range(2):
            tok0 = ps * 1024
            with ExitStack() as st:
                actT = sb(st, "yTs", [128, 32, 1024], BF16)
                for q in range(8):
                    P.dma("sp", actT[:, q * 4:(q + 1) * 4, :],
                          yT[q * 512:(q + 1) * 512, tok0:tok0 + 1024].rearrange("(c p) t -> p c t", p=128),
                          "ya%d" % (q % 4), writes=[("yTs", b) for b in range(8)] if q == 0 else [("yTs_", q)])
                P.barrier()
                evh = make_ev(st, 6, [128, 512], F32, "evh")
                W = wfull["w_out", l]
                panels = [dict(srcs=[(W[:, i * 512:(i + 1) * 512], 0)], ncols=512, mode="tok",
                               epi=epi_resid(evh, hsrc, tok0), n0=i * 512, wkeys=[("wfull", "w_out", l)]) for i in range(8)]
                linear(actT, "yTs", 32, 1024, panels)
                P.barrier()

        if KSTOP < 6:
            continue
        for ps in range(2):
            tok0 = ps * 1024
            with ExitStack() as st:
                actT = sb(st, "hnT", [128, 32, 1024], BF16)
                with ExitStack() as st2:
                    norm_T(st2, hbuf, tok0, 8, nffn[l:l + 1, :], actT, "hnT")
                    P.barrier()
                sg = make_ev(st, 4, [128, 512], F32, "sg")
                hd = make_ev(st, 4, [128, 512], BF16, "hd")
                Wg = wfull["w_gate", l]
                Wu = wfull["w_up", l]

                def epi_ffn(pn, g, bk0, tok0=tok0):
                    f0 = pn["n0"]
                    for j in range(2):
                        s_, ks = sg()
                        h_, kh = hd()
                        P.op("act", lambda e, s_=s_, b=banks[bk0 + j]: e.activation(out=s_, in_=b, func=AF.Silu),
                             reads=[BK(bk0 + j)], writes=[ks])
                        P.op("dve", lambda e, s_=s_, h_=h_, b=banks[bk0 + 2 + j]: e.tensor_tensor(out=h_, in0=b, in1=s_, op=ALU.mult),
                             reads=[BK(bk0 + 2 + j), ks], writes=[kh])
                        P.dma("sp", hidT[f0 + j * 128:f0 + (j + 1) * 128, tok0 + g * 512:tok0 + (g + 1) * 512], h_,
                              "st_%s_%d" % (kh[0], kh[2]), reads=[kh])
                panels = [dict(srcs=[(Wg[:, f * 256:(f + 1) * 256], 0), (Wu[:, f * 256:(f + 1) * 256], 256)], ncols=512,
                               mode="feat", epi=epi_ffn, n0=f * 256,
                               wkeys=[("wfull", "w_gate", l), ("wfull", "w_up", l)]) for f in range(D_FF // 256)]
                linear(actT, "hnT", 32, 1024, panels)
                P.barrier()

        if KSTOP < 7:
            continue
        for ps in range(4):
            tok0 = ps * 512
            with ExitStack() as st:
                actT = sb(st, "hdT", [128, 86, 512], BF16)
                for q in range(22):
                    c0 = q * 4
                    n = min(4, 86 - c0)
                    P.dma("sp", actT[:, c0:c0 + n, :],
                          hidT[c0 * 128:(c0 + n) * 128, tok0:tok0 + 512].rearrange("(c p) t -> p c t", p=128),
                          "ya%d" % (q % 4), writes=[("hdT_", q)])
                P.barrier()
                evh = make_ev(st, 6, [128, 512], F32, "evh")
                W = wfull["w_down", l]
                panels = [dict(srcs=[(W[:, i * 512:(i + 1) * 512], 0)], ncols=512, mode="tok",
                               epi=epi_resid(evh, hbuf, tok0), n0=i * 512, wkeys=[("wfull", "w_down", l)]) for i in range(8)]
                linear(actT, "hdT", 86, 512, panels)
                P.barrier()

        if KSTOP < 8:
            continue
        for ps in range(2):
            tok0 = ps * 1024
            with ExitStack() as st:
                actT = sb(st, "gnT", [128, 32, 1024], BF16)
                with ExitStack() as st2:
                    norm_T(st2, hbuf, tok0, 8, nple[l:l + 1, :], actT, "gnT")
                    P.barrier()
                ev16 = make_ev(st, 4, [128, 512], BF16, "ev16")
                W = wfull["ple_gate_down", l]
                panels = [dict(srcs=[(W, 0)], ncols=256, mode="feat", epi=epi_store_feat(ev16, g1T, 0, tok0), n0=0,
                               wkeys=[("wfull", "ple_gate_down", l)])]
                linear(actT, "gnT", 32, 1024, panels)
                P.barrier()
        for ps in range(4):
            tok0 = ps * 512
            with ExitStack() as st:
                g1s = sb(st, "g1s", [128, 2, 512], BF16)
                pTs = sb(st, "pTs", [128, 2, 512], BF16)
                wu = sb(st, "wu", [128, 2, D], BF16)
                wp = sb(st, "wp", [128, 2, D], BF16)
                P.dma("sp", g1s, g1T[:, tok0:tok0 + 512].rearrange("(c p) t -> p c t", p=128), "ya0", writes=["g1s"])
                P.dma("pool", pTs, pT[l * PLE:(l + 1) * PLE, tok0:tok0 + 512].rearrange("(c p) t -> p c t", p=128), "pp",
                      writes=["pTs"])
                for c in range(2):
                    P.dma("pool", wu[:, c, :], wfull["ple_gate_up", l][c * 128:(c + 1) * 128, :], "pp",
                          reads=[("wbig", l, j_) for j_ in WCH["ple_gate_up"]], writes=[("wu", c)])
                    P.dma("pool", wp[:, c, :], wfull["ple_proj", l][c * 128:(c + 1) * 128, :], "pp",
                          reads=[("wbig", l, j_) for j_ in WCH["ple_proj"]], writes=[("wp", c)])
                sgt = make_ev(st, 2, [128, 512], F32, "sgt")
                evh = make_ev(st, 4, [128, 512], F32, "evh")
                for pn in range(8):
                    n0 = pn * 512
                    for j in range(4):
                        bka = (pn * 4 + j) % 4
                        bkb = 4 + bka
                        for c in range(2):
                            P.op("pe", lambda e, bka=bka, j=j, c=c, n0=n0: e.matmul(banks[bka], g1s[:, c, j * 128:(j + 1) * 128],
                                                                                   wu[:, c, n0:n0 + 512], start=(c == 0), stop=(c == 1)),
                                 reads=["g1s", "pTs", ("wu", 0), ("wu", 1), ("wp", 0), ("wp", 1)], writes=[BK(bka)])
                        for c in range(2):
                            P.op("pe", lambda e, bkb=bkb, j=j, c=c, n0=n0: e.matmul(banks[bkb], pTs[:, c, j * 128:(j + 1) * 128],
                                                                                   wp[:, c, n0:n0 + 512], start=(c == 0), stop=(c == 1)),
                                 reads=["pTs", ("wp", c)], writes=[BK(bkb)])
                        s_, ks = sgt()
                        t, k = evh()
                        r0 = tok0 + j * 128
                        P.dma("sp", t, hbuf[r0:r0 + 128, n0:n0 + 512], "ld_%s_%d" % (k[0], k[2]), writes=[k])
                        P.op("act", lambda e, s_=s_, bka=bka: e.activation(out=s_, in_=banks[bka], func=AF.Sigmoid),
                             reads=[BK(bka)], writes=[ks])
                        P.op("dve", lambda e, s_=s_, bkb=bkb: e.tensor_tensor(out=s_, in0=banks[bkb], in1=s_, op=ALU.mult),
                             reads=[BK(bkb), ks], writes=[ks])
                        P.op("dve", lambda e, s_=s_, t=t: e.tensor_tensor(out=t, in0=t, in1=s_, op=ALU.add),
                             reads=[ks, k], writes=[k])
                        P.dma("sp", hbuf[r0:r0 + 128, n0:n0 + 512], t, "st_%s_%d" % (k[0], k[2]), reads=[k])
                P.barrier()

    with ExitStack() as st:
        wB = sb(st, "fwB", [128, D], F32)
        ht = [sb(st, "fht", [128, D], F32) for _ in range(2)]
        ot = [sb(st, "fot", [128, D], F32) for _ in range(2)]
        ss = sb(st, "fss", [128, 4], F32)
        P.dma("sp", wB, nfin.partition_broadcast(128), "nw", writes=["fwB"])
        fin = []
        for b in range(NBLK):
            h, o = ht[b % 2], ot[b % 2]
            hk, ok = ("fht", b % 2), ("fot", b % 2)
            P.dma("sp", h, hbuf[b * 128:(b + 1) * 128, :], "ht%d" % (b % 2), writes=[hk])
            P.op("act", lambda e, h=h, o=o: e.activation(out=o, in_=h, func=AF.Square, accum_out=ss[:, 0:1]),
                 reads=[hk], writes=[ok, "fss0"])
            P.op("act", lambda e: e.activation(out=ss[:, 1:2], in_=ss[:, 0:1], func=AF.Sqrt, bias=epsr[:, 0:1], scale=1.0 / D),
                 reads=["fss0", "epsr"], writes=["fss1"])
            P.op("dve", lambda e: e.reciprocal(out=ss[:, 2:3], in_=ss[:, 1:2]), reads=["fss1"], writes=["fss2"])
            P.op("dve", lambda e, h=h, o=o: e.scalar_tensor_tensor(out=o, in0=h, scalar=ss[:, 2:3], in1=wB, op0=ALU.mult,
                                                                    op1=ALU.mult), reads=[hk, "fss2", "fwB"], writes=[ok])
            fin.append(P.dma("sp", out[b * 128:(b + 1) * 128, :], o, "fo%d" % (b % 2), reads=[ok]))
        P.op("sp", lambda e: e.nop(), extra=fin, writes=["fin"])

    names = P.plan()
    with ExitStack() as st:
        sems = {n: st.enter_context(nc.semaphore(n)) for n in names}
        block = st.enter_context(nc.Block())
        P.emit(block, sems)
    return nc, {e: len(P.ops[e]) for e in ENGINES}


def _consts(j):
    ident = np.eye(128, dtype=np.float32)
    tril = np.tril(np.ones((128, 128), np.float32))
    strict = np.tril(np.ones((128, 128), np.float32), -1)
    if j == 0:
        m2 = np.concatenate([strict, np.zeros((128, 128), np.float32)], 1)
        sel = np.tile(np.array([[0.0, 1.0]], np.float32), (128, 1))
    else:
        m2 = np.concatenate([np.ones((128, 128), np.float32), strict], 1)
        sel = np.tile(np.array([[1.0, 0.0]], np.float32), (128, 1))
    invc = np.zeros((4, 128), np.float32)
    pos = np.arange(128) + (0 if j == 0 else 128)
    for g, w in enumerate((2, 4, 8, 16)):
        invc[g] = 1.0 / np.minimum(pos + 1, w)
    invc = np.tile(invc.reshape(1, 512), (128, 1))
    return dict(c_ident=ident, c_tril=tril, c_m2=np.ascontiguousarray(m2), c_sel=sel, c_invc=np.ascontiguousarray(invc))


_CACHE = {}


def kernel(x, p, norm_mix_w, w_in, pool_w, pool_scale, sgu_norm_w, sgu_w, sgu_b, w_out, norm_ffn_w, w_gate, w_up,
           w_down, norm_ple_w, ple_gate_down, ple_gate_up, ple_proj, final_norm_w, _depth=DEPTH):
    f = lambda a: np.ascontiguousarray(np.asarray(a, dtype=np.float32))
    x = f(x)
    p = f(p)
    B = x.shape[0]
    if _depth not in _CACHE:
        _CACHE[_depth] = build_program(_depth)[0]
    nc = _CACHE[_depth]
    small = dict(
        pool_w=f(pool_w).reshape(DEPTH * 4 * 256, 256), pool_scale=f(pool_scale), sgu_norm_w=f(sgu_norm_w),
        sgu_w=f(sgu_w).reshape(DEPTH * 8 * 128, 128), sgu_b=f(sgu_b).reshape(DEPTH, 1024),
        norm_mix_w=f(norm_mix_w), norm_ffn_w=f(norm_ffn_w), norm_ple_w=f(norm_ple_w),
        final_norm_w=f(final_norm_w).reshape(1, D))
    MI = 1 << 20
    catB = np.concatenate([f(a).reshape(DEPTH, -1) for a in (w_in, w_out, w_down, ple_proj)], axis=1).reshape(DEPTH, 12, 8, MI)
    catA = np.concatenate([f(a).reshape(DEPTH, -1) for a in (w_gate, w_up, ple_gate_down, ple_gate_up)], axis=1).reshape(DEPTH, 11, 8, MI)
    wcat = np.concatenate([catB, catA], axis=1)
    del catA, catB
    in_maps = []
    for c in range(8):
        b, j = c // 2, c % 2
        xc = x[b].reshape(32, 128, D)[j::2].reshape(NTOK, D)
        pc = p[:, b].reshape(DEPTH, 32, 128, PLE)[:, j::2].reshape(DEPTH, NTOK, PLE)
        pTc = np.ascontiguousarray(pc.transpose(0, 2, 1)).reshape(DEPTH * PLE, NTOK)
        m = dict(small)
        m["wsh"] = np.ascontiguousarray(wcat[:, :, c, :]).reshape(DEPTH * 23 * 512, 2048)
        m["x"] = np.ascontiguousarray(xc)
        m["pT"] = pTc
        m.update(_consts(j))
        in_maps.append(m)
    res = run_bass_kernel_spmd(nc, in_maps, core_ids=list(range(8)))
    outp = np.empty((B, SEQ, D), np.float32)
    for c in range(8):
        b, j = c // 2, c % 2
        outp[b].reshape(32, 128, D)[j::2] = np.asarray(res.results[c]["out"]).reshape(16, 128, D)
    return outp
```
